# Optimizing a Trainium2 kernel written in Bass

```python
import math
import jax, jax.numpy as jnp
from jax import lax
import numpy as np

D_MODEL = 1024
BATCH = 4
SEQ = 8192
DEPTH = 4

HEAD_DIM = 64
N_BRANCH = 4
BRANCH_WIDTH = D_MODEL // N_BRANCH
POOL_WINDOWS = (2, 4, 8, 16)
N_POOL_GROUPS = len(POOL_WINDOWS)
POOL_GROUP = BRANCH_WIDTH // N_POOL_GROUPS
CONV_K = 3
MOBA_HEADS = BRANCH_WIDTH // HEAD_DIM
MOBA_BLOCK = 256
MOBA_TOPK = 3
MOBA_Q_CHUNK = 64
FOX_HEADS = BRANCH_WIDTH // HEAD_DIM
FOX_Q_BLOCK = 128
D_FF = 4 * D_MODEL
ROPE_THETA = 500000.0
ROPE_DIM = HEAD_DIM // 4
ALPHA = (2 * DEPTH) ** 0.25
BETA = (8 * DEPTH) ** -0.25
LN_EPS = 1e-5
FORGET_BIAS_INIT = 2.0
IN_SPLIT_WIDTHS = (BRANCH_WIDTH,) * 10 + (FOX_HEADS, N_BRANCH * D_MODEL)
IN_WIDTH = 10 * BRANCH_WIDTH + FOX_HEADS + N_BRANCH * D_MODEL
FORGET_COL0 = 10 * BRANCH_WIDTH

kernel_name = "hybrid_parallel_pool_conv_moba_fox_block"


def _in_split_points():
    pts, acc = [], 0
    for w in IN_SPLIT_WIDTHS[:-1]:
        acc += w
        pts.append(acc)
    return pts


def layer_norm(x):
    xf = x.astype(jnp.float32)
    mu = jnp.mean(xf, axis=-1, keepdims=True)
    var = jnp.mean(jnp.square(xf - mu), axis=-1, keepdims=True)
    return ((xf - mu) * lax.rsqrt(var + LN_EPS)).astype(x.dtype)


def rotary_tables(positions):
    inv = ROPE_THETA ** (-jnp.arange(0, ROPE_DIM, 2, dtype=jnp.float32) / ROPE_DIM)
    ang = positions.astype(jnp.float32)[..., None] * inv
    return jnp.cos(ang)[:, None], jnp.sin(ang)[:, None]


def apply_partial_rotary(t, cos, sin):
    half = ROPE_DIM // 2
    t1, t2, rest = t[..., :half], t[..., half:ROPE_DIM], t[..., ROPE_DIM:]
    r1 = (t1 * cos - t2 * sin).astype(t.dtype)
    r2 = (t2 * cos + t1 * sin).astype(t.dtype)
    return jnp.concatenate([r1, r2, rest], axis=-1)


def to_heads(t, n_heads):
    b, s, _ = t.shape
    return t.reshape(b, s, n_heads, HEAD_DIM).transpose(0, 2, 1, 3)


def from_heads(t):
    b, h, s, d = t.shape
    return t.transpose(0, 2, 1, 3).reshape(b, s, h * d)


def pool_mixer(p, w_pool, pool_scale):
    s = p.shape[1]
    t = jnp.arange(s)
    groups = jnp.split(p, N_POOL_GROUPS, axis=-1)
    outs = []
    for g, w in zip(groups, POOL_WINDOWS):
        cs = jnp.cumsum(g.astype(jnp.float32), axis=1)
        cs_prev = jnp.pad(cs, ((0, 0), (w, 0), (0, 0)))[:, :s]
        cnt = jnp.minimum(t + 1, w).astype(jnp.float32)[None, :, None]
        outs.append(((cs - cs_prev) / cnt).astype(p.dtype) - g)
    pooled = jnp.stack(outs, axis=2)
    mixed = jnp.einsum('bsgc,gcd->bsgd', pooled, w_pool)
    return mixed.reshape(p.shape) * pool_scale


def short_conv_mixer(gate_b, gate_c, h, conv_w):
    v = gate_c * h
    z = lax.conv_general_dilated(
        v, conv_w[:, None, :], window_strides=(1,), padding=[(CONV_K - 1, 0)],
        dimension_numbers=('NWC', 'WIO', 'NWC'), feature_group_count=v.shape[-1])
    return gate_b * z


def moba_attention(q, k, v):
    b, h, s, hd = q.shape
    nblk = -(-s // MOBA_BLOCK)
    s_pad = nblk * MOBA_BLOCK
    pad = ((0, 0), (0, 0), (0, s_pad - s), (0, 0))
    kb = jnp.pad(k, pad).reshape(b, h, nblk, MOBA_BLOCK, hd)
    vb = jnp.pad(v, pad).reshape(b, h, nblk, MOBA_BLOCK, hd)
    kbar = jnp.mean(kb.astype(jnp.float32), axis=3)
    n_sel = min(MOBA_TOPK, nblk)
    scale = HEAD_DIM ** -0.5
    bi = jnp.arange(b)[:, None, None, None]
    hi = jnp.arange(h)[None, :, None, None]
    blk_ids = jnp.arange(nblk)
    n_chunks = s // MOBA_Q_CHUNK

    def chunk(ci):
        t0 = ci * MOBA_Q_CHUNK
        qc = lax.dynamic_slice_in_dim(q, t0, MOBA_Q_CHUNK, axis=2)
        tq = t0 + jnp.arange(MOBA_Q_CHUNK)
        own = t0 // MOBA_BLOCK
        score = jnp.einsum('bhqd,bhnd->bhqn', qc.astype(jnp.float32), kbar)
        score = jnp.where(blk_ids < own, score, -jnp.inf)
        _, idx = lax.top_k(score, n_sel)
        sel_valid = idx < own
        k_sel = kb[bi, hi, idx]
        v_sel = vb[bi, hi, idx]
        l_sel = jnp.einsum('bhqd,bhqkld->bhqkl', qc, k_sel).astype(jnp.float32) * scale
        l_sel = jnp.where(sel_valid[..., None], l_sel, -jnp.inf)
        k_own = lax.dynamic_index_in_dim(kb, own, axis=2, keepdims=False)
        v_own = lax.dynamic_index_in_dim(vb, own, axis=2, keepdims=False)
        l_own = jnp.einsum('bhqd,bhld->bhql', qc, k_own).astype(jnp.float32) * scale
        pos_own = own * MOBA_BLOCK + jnp.arange(MOBA_BLOCK)
        l_own = jnp.where(pos_own[None, :] <= tq[:, None], l_own, -jnp.inf)
        n_k = n_sel * MOBA_BLOCK
        logits = jnp.concatenate([l_sel.reshape(b, h, MOBA_Q_CHUNK, n_k), l_own], axis=-1)
        p = jax.nn.softmax(logits, axis=-1)
        p_sel = p[..., :n_k].reshape(b, h, MOBA_Q_CHUNK, n_sel, MOBA_BLOCK).astype(v.dtype)
        p_own = p[..., n_k:].astype(v.dtype)
        return (jnp.einsum('bhqkl,bhqkld->bhqd', p_sel, v_sel)
                + jnp.einsum('bhql,bhld->bhqd', p_own, v_own))

    out = lax.map(chunk, jnp.arange(n_chunks))
    return out.transpose(1, 2, 0, 3, 4).reshape(b, h, s, hd)


def forgetting_attention(q, k, v, log_f):
    b, h, s, hd = q.shape
    cum = jnp.cumsum(log_f, axis=-1)
    scale = HEAD_DIM ** -0.5
    s_pos = jnp.arange(s)

    def block(qi):
        t0 = qi * FOX_Q_BLOCK
        qb = lax.dynamic_slice_in_dim(q, t0, FOX_Q_BLOCK, axis=2)
        cq = lax.dynamic_slice_in_dim(cum, t0, FOX_Q_BLOCK, axis=2)
        tq = t0 + jnp.arange(FOX_Q_BLOCK)
        logits = (jnp.einsum('bhqd,bhkd->bhqk', qb, k).astype(jnp.float32) * scale
                  + cq[..., :, None] - cum[..., None, :])
        logits = jnp.where(s_pos[None, :] <= tq[:, None], logits, -jnp.inf)
        p = jax.nn.softmax(logits, axis=-1)
        return jnp.einsum('bhqk,bhkd->bhqd', p.astype(v.dtype), v)

    out = lax.map(block, jnp.arange(s // FOX_Q_BLOCK))
    return out.transpose(1, 2, 0, 3, 4).reshape(b, h, s, hd)


def hybrid_layer(x, cond_act, cos, sin, w_ada, b_ada, w_in, b_in, w_pool, pool_scale, conv_w,
                 w_branch, w_o, ln1_g, ln1_b, w_up, b_up, w_down, ln2_g, ln2_b):
    ada = (cond_act @ w_ada + b_ada)[:, None, :]
    sh1, sc1, g1, sh2, sc2, g2 = jnp.split(ada, 6, axis=-1)

    u = layer_norm(x) * (1 + sc1) + sh1
    proj = u @ w_in + b_in
    (p_in, c_b, c_c, c_h, mq, mk, mv, fq, fk, fv, f_logit, gate_logit) = jnp.split(
        proj, _in_split_points(), axis=-1)

    y_pool = pool_mixer(p_in, w_pool, pool_scale)
    y_conv = short_conv_mixer(c_b, c_c, c_h, conv_w)
    q_m = apply_partial_rotary(to_heads(mq, MOBA_HEADS), cos, sin)
    k_m = apply_partial_rotary(to_heads(mk, MOBA_HEADS), cos, sin)
    y_moba = from_heads(moba_attention(q_m, k_m, to_heads(mv, MOBA_HEADS)))
    log_f = jax.nn.log_sigmoid(f_logit.astype(jnp.float32)).transpose(0, 2, 1)
    y_fox = from_heads(forgetting_attention(to_heads(fq, FOX_HEADS), to_heads(fk, FOX_HEADS),
                                            to_heads(fv, FOX_HEADS), log_f))

    gates = jax.nn.sigmoid(gate_logit)
    branches = (y_pool, y_conv, y_moba, y_fox)
    merged = None
    for n in range(N_BRANCH):
        term = gates[..., n * D_MODEL:(n + 1) * D_MODEL] * (branches[n] @ w_branch[n])
        merged = term if merged is None else merged + term
    mixed = merged @ w_o
    x = layer_norm(ALPHA * x + g1 * mixed) * ln1_g + ln1_b

    u2 = layer_norm(x) * (1 + sc2) + sh2
    hdn = jnp.square(jax.nn.relu(u2 @ w_up + b_up))
    x = layer_norm(ALPHA * x + g2 * (hdn @ w_down)) * ln2_g + ln2_b
    return x


def setup_inputs(seed: int = 0) -> dict:
    key = jax.random.key(seed)
    ks = jax.random.split(key, 24)
    f32 = jnp.float32
    nrm = lambda k, shape, s: jax.random.normal(k, shape, f32) * s
    x = jax.random.normal(ks[0], (BATCH, SEQ, D_MODEL), f32)
    c = jax.random.normal(ks[1], (BATCH, D_MODEL), f32)
    offsets = jax.random.randint(ks[2], (BATCH, 1), 0, 1024, dtype=jnp.int32)
    positions = jnp.arange(SEQ, dtype=jnp.int32)[None, :] + offsets
    w_ada = nrm(ks[3], (DEPTH, D_MODEL, 6 * D_MODEL), D_MODEL ** -0.5)
    b_ada = nrm(ks[4], (DEPTH, 6 * D_MODEL), 0.01)
    w_in = nrm(ks[5], (DEPTH, D_MODEL, IN_WIDTH), D_MODEL ** -0.5)
    b_in_raw = nrm(ks[6], (DEPTH, IN_WIDTH), 0.01)
    forget_b = FORGET_BIAS_INIT + nrm(ks[7], (DEPTH, FOX_HEADS), 0.1)
    b_in = jnp.concatenate([b_in_raw[:, :FORGET_COL0], forget_b,
                            b_in_raw[:, FORGET_COL0 + FOX_HEADS:]], axis=-1)
    w_pool = nrm(ks[8], (DEPTH, N_POOL_GROUPS, POOL_GROUP, POOL_GROUP), POOL_GROUP ** -0.5)
    pool_scale = 1.0 + nrm(ks[9], (DEPTH, BRANCH_WIDTH), 0.02)
    conv_w = nrm(ks[10], (DEPTH, CONV_K, BRANCH_WIDTH), CONV_K ** -0.5)
    w_branch = nrm(ks[11], (DEPTH, N_BRANCH, BRANCH_WIDTH, D_MODEL), BRANCH_WIDTH ** -0.5)
    w_o = nrm(ks[12], (DEPTH, D_MODEL, D_MODEL), BETA * D_MODEL ** -0.5)
    ln1_g = 1.0 + nrm(ks[13], (DEPTH, D_MODEL), 0.02)
    ln1_b = nrm(ks[14], (DEPTH, D_MODEL), 0.02)
    w_up = nrm(ks[15], (DEPTH, D_MODEL, D_FF), D_MODEL ** -0.5)
    b_up = nrm(ks[16], (DEPTH, D_FF), 0.01)
    w_down = nrm(ks[17], (DEPTH, D_FF, D_MODEL), BETA * D_FF ** -0.5)
    ln2_g = 1.0 + nrm(ks[18], (DEPTH, D_MODEL), 0.02)
    ln2_b = nrm(ks[19], (DEPTH, D_MODEL), 0.02)
    return {"x": x, "c": c, "positions": positions, "w_ada": w_ada, "b_ada": b_ada,
            "w_in": w_in, "b_in": b_in, "w_pool": w_pool, "pool_scale": pool_scale,
            "conv_w": conv_w, "w_branch": w_branch, "w_o": w_o, "ln1_g": ln1_g, "ln1_b": ln1_b,
            "w_up": w_up, "b_up": b_up, "w_down": w_down, "ln2_g": ln2_g, "ln2_b": ln2_b}


def reference(x, c, positions, w_ada, b_ada, w_in, b_in, w_pool, pool_scale, conv_w, w_branch,
              w_o, ln1_g, ln1_b, w_up, b_up, w_down, ln2_g, ln2_b):
    cos, sin = rotary_tables(positions)
    cond_act = jax.nn.silu(c)
    for l in range(DEPTH):
        x = hybrid_layer(x, cond_act, cos, sin, w_ada[l], b_ada[l], w_in[l], b_in[l], w_pool[l],
                         pool_scale[l], conv_w[l], w_branch[l], w_o[l], ln1_g[l], ln1_b[l],
                         w_up[l], b_up[l], w_down[l], ln2_g[l], ln2_b[l])
    return x
```

```python
import contextlib
import math
import numpy as np
import concourse.bass as bass
import concourse.mybir as mybir
from concourse.bass_utils import run_bass_kernel_spmd

F32 = mybir.dt.float32
BF16 = mybir.dt.bfloat16
I32 = mybir.dt.int32
AF = mybir.ActivationFunctionType
ALU = mybir.AluOpType
AX = mybir.AxisListType

D = 1024
SEQ = 8192
BATCH = 4
DEPTH = 4
DFF = 4096
HD = 64
IN_WIDTH = 6660
ALPHA = (2 * DEPTH) ** 0.25
LN_EPS = 1e-5
ROPE_THETA = 500000.0
T = SEQ
G = 512
NG = T // G
NTL = T // 128
NC1 = 20 * 128 + 512 + 4
NCT = 20
NEG = -30000.0
ENGINES = ("tensor", "vector", "scalar", "gpsimd", "sync")


class Op:
    __slots__ = ("eng", "fn", "deps", "is_dma", "semkey", "ms", "needed")

    def __init__(self, eng, fn, is_dma, semkey):
        self.eng = eng
        self.fn = fn
        self.deps = []
        self.is_dma = is_dma
        self.semkey = semkey
        self.ms = None
        self.needed = False


class Sched:
    def __init__(self, nc, stack):
        self.nc = nc
        self.stack = stack
        self.pending = {e: [] for e in ENGINES}
        self.last_w = {}
        self.readers = {}
        self.sem_count = {}
        self.sems = {}
        self.last_by_key = {}
        self.waited = {e: {} for e in ENGINES}
        self.n_inst = 0
        self.group_keys = set()
        self.trace = {}
        import os
        for i in range(int(os.environ.get("SEM_SKIP", "0"))):
            self.stack.enter_context(nc.semaphore("dummy%d" % i))

    def op(self, eng, fn, reads=(), writes=(), dma=False, semkey=None):
        semkey = (semkey or ("dma", eng)) if dma else ("eng", eng)
        o = Op(eng, fn, dma, semkey)
        deps = set()
        for r in reads:
            w = self.last_w.get(r)
            if w is not None:
                deps.add(w)
        for r in writes:
            w = self.last_w.get(r)
            if w is not None:
                deps.add(w)
            for rd in self.readers.get(r, ()):
                deps.add(rd)
        for r in reads:
            self.readers.setdefault(r, []).append(o)
        for r in writes:
            self.last_w[r] = o
            self.readers[r] = []
        deps.discard(o)
        o.deps = list(deps)
        for d in o.deps:
            d.needed = True
        self.pending[eng].append(o)
        self.last_by_key[semkey] = o
        return o

    def barrier(self):
        targets = list(self.last_by_key.values())
        for t in targets:
            t.needed = True
        for e in ENGINES:
            b = Op(e, None, False, ("eng", e))
            b.deps = targets
            self.pending[e].append(b)
        self.last_w = {}
        self.readers = {}

    def flush(self):
        nc = self.nc
        for e in ENGINES:
            for o in self.pending[e]:
                if o.fn is None:
                    continue
                if o.is_dma:
                    self.sem_count[o.semkey] = self.sem_count.get(o.semkey, 0) + 16
                    o.ms = self.sem_count[o.semkey]
                elif o.needed:
                    self.sem_count[o.semkey] = self.sem_count.get(o.semkey, 0) + 1
                    o.ms = self.sem_count[o.semkey]
        for gk in self.group_keys:
            grp = [o for e in ENGINES for o in self.pending[e] if o.fn is not None and o.semkey == gk]
            if grp:
                mx = max(o.ms for o in grp)
                for o in grp:
                    o.ms = mx
        for k in self.sem_count:
            if k not in self.sems:
                name = "s%d_" % len(self.sems) + "".join(ch if ch.isalnum() else "_" for ch in str(k))[:40]
                self.sems[k] = self.stack.enter_context(nc.semaphore(name))
        with nc.Block() as block:
            for e in ENGINES:
                if self.pending[e]:
                    getattr(block, e)(self._mk_body(e, self.pending[e]))
        self.pending = {e: [] for e in ENGINES}

    def _mk_body(self, e, ops):
        sems = self.sems
        waited = self.waited[e]

        tr = self.trace.setdefault(e, [])

        def body(eng):
            for o in ops:
                tr.append(o)
                need = {}
                for d in o.deps:
                    if d.ms is None:
                        continue
                    if need.get(d.semkey, 0) < d.ms:
                        need[d.semkey] = d.ms
                for k, v in need.items():
                    if waited.get(k, 0) >= v:
                        continue
                    eng.wait_ge(sems[k], v)
                    waited[k] = v
                    self.n_inst += 1
                if o.fn is None:
                    continue
                ins = o.fn(eng)
                self.n_inst += 1
                if o.is_dma:
                    ins.then_inc(sems[o.semkey], 16)
                elif o.ms is not None:
                    ins.then_inc(sems[o.semkey], 1)
        return body


def _cols1():
    cols = []
    cols += list(range(0, 256))
    cols += list(range(256, 1024))
    for base in (1024, 1280):
        for pair in range(2):
            for r in range(128):
                hh, dd = r // 64, r % 64
                src = ((dd + 8) % 16) if dd < 16 else dd
                cols.append(base + (pair * 2 + hh) * 64 + src)
    cols += list(range(1024, 1280))
    cols += list(range(1280, 1536))
    cols += list(range(1792, 2048))
    cols += list(range(2048, 2304))
    cols += list(range(1536, 1792))
    cols += list(range(2304, 2560))
    cols += list(range(2560, 2564))
    assert len(cols) == NC1
    return np.asarray(cols, dtype=np.int64)


def _consts():
    c = np.zeros((128, 1024), np.float32)
    p = np.arange(128)
    c[:, 0:128] = np.eye(128, dtype=np.float32)
    c[:, 128:256] = 1.0
    c[:, 256:384] = (p[:, None] <= p[None, :]).astype(np.float32)
    d = p % 64
    inv = ROPE_THETA ** (-(2.0 * (d % 8)) / 16.0)
    c[:, 384] = np.where(d < 16, inv, 0.0)
    c[:, 385] = np.where(d < 8, -1.0, np.where(d < 16, 1.0, 0.0))
    w_lo = np.array([2.0, 8.0])
    w_hi = np.array([4.0, 16.0])
    for t in range(2):
        w = np.where(p < 64, w_lo[t], w_hi[t])
        c[:, 386 + t] = 1.0 / w
        for j in range(16):
            c[:, 388 + t * 16 + j] = 1.0 / np.minimum(j + 1.0, w)
    c[:, 420] = LN_EPS
    return c


def _onehot():
    oh = np.zeros((32, T), np.float32)
    for n in range(32):
        oh[n, n * 256:(n + 1) * 256] = 1.0
    return oh


class Prog:
    def __init__(self, depth=DEPTH, debug=False, phases="01234"):
        self.depth = depth
        self.debug = debug
        self.phases = phases
        self.nc = bass.Bass("TRN2", target_bir_lowering=False)
        self.build()

    def din(self, name, shape, dt=F32):
        return self.nc.dram_tensor(name, list(shape), dt, kind="ExternalInput").ap()

    def dscr(self, name, shape, dt=F32):
        if self.debug:
            return self.nc.dram_tensor(name, list(shape), dt, kind="ExternalOutput").ap()
        return self.nc.dram_tensor(name, list(shape), dt).ap()

    def uniq(self, name):
        self._uid = getattr(self, "_uid", 0) + 1
        return "%s_u%d" % (name, self._uid)

    def mm(self, out, lhsT, rhs, start, stop, reads, writes):
        self.S.op("tensor", lambda e: e.matmul(out, lhsT, rhs, start=start, stop=stop), reads, writes)

    def tr(self, out, in_, reads, writes):
        ident = self.ident_b
        self.S.op("tensor", lambda e: e.transpose(out, in_, ident), reads, writes)

    def act(self, out, in_, func, reads, writes, **kw):
        self.S.op("scalar", lambda e: e.activation(out, in_, func, **kw), reads, writes)

    def tt(self, eng, out, in0, in1, op, reads, writes):
        self.S.op(eng, lambda e: e.tensor_tensor(out, in0, in1, op), reads, writes)

    def ts(self, eng, out, in0, s1, s2, op0, op1, reads, writes):
        if op1 is None:
            self.S.op(eng, lambda e: e.tensor_scalar(out, in0, s1, None, op0), reads, writes)
        else:
            self.S.op(eng, lambda e: e.tensor_scalar(out, in0, s1, s2, op0, op1), reads, writes)

    def stt(self, eng, out, in0, scalar, in1, op0, op1, reads, writes):
        self.S.op(eng, lambda e: e.scalar_tensor_tensor(out, in0, scalar, in1, op0, op1), reads, writes)

    def cp(self, eng, out, in_, reads, writes):
        self.S.op(eng, lambda e: e.tensor_copy(out, in_), reads, writes)

    def ms(self, eng, ap, val, reads, writes):
        self.S.op(eng, lambda e: e.memset(ap, val), reads, writes)

    def dma(self, out, in_, reads, writes, q="sync", sk=None, grp=False):
        if sk is None:
            sk = "par" if q == "sync" else "wmisc"
            grp = True
        key = ("dma", q, sk)
        if grp:
            self.S.group_keys.add(key)
        self.S.op(q, lambda e: e.dma_start(out=out, in_=in_), reads, writes, dma=True, semkey=key)

    def build(self):
        nc = self.nc
        L = self.depth
        d = self.d = {}
        d["x"] = self.din("x", [T, D])
        d["c"] = self.din("c", [128, 8])
        d["pos"] = self.din("pos", [1, T], I32)
        d["consts"] = self.din("consts", [128, 1024])
        d["onehot"] = self.din("onehot", [32, T])
        d["w_ada"] = self.din("w_ada", [L, 128, 8, 6 * D])
        d["b_ada"] = self.din("b_ada", [L, 1, 6 * D])
        d["w1"] = self.din("w1", [L, 128, 8, NC1])
        d["b1T"] = self.din("b1T", [L, 128, NCT])
        d["b1row"] = self.din("b1row", [L, 1, 516])
        d["w2"] = self.din("w2", [L, 128, 8, 4096])
        d["b2T"] = self.din("b2T", [L, 128, 32])
        d["wpb"] = self.din("wpb", [L, 128, 2, 128])
        d["pscT"] = self.din("pscT", [L, 128, 2])
        d["convT"] = self.din("convT", [L, 128, 2, 3])
        d["wbr"] = self.din("wbr", [L, 128, 8, D])
        d["wo"] = self.din("wo", [L, 128, 8, D])
        d["lnrow"] = self.din("lnrow", [L, 4, D])
        d["wup"] = self.din("wup", [L, 128, 8, DFF])
        d["bupT"] = self.din("bupT", [L, 128, 32])
        d["wdn"] = self.din("wdn", [L, 128, 32, D])
        d["out"] = nc.dram_tensor("out", [T, D], F32, kind="ExternalOutput").ap()
        d["rotC"] = self.dscr("rotC", [128, T])
        d["rotS"] = self.dscr("rotS", [128, T])
        d["ada_row"] = self.dscr("ada_row", [1, 6 * D])
        d["uT"] = self.dscr("uT", [128, 8, T], BF16)
        d["qk"] = self.dscr("qk", [8, 128, T], BF16)
        d["v"] = self.dscr("v", [T, 512], BF16)
        d["yT"] = self.dscr("yT", [8, 128, T], BF16)
        d["x1"] = self.dscr("x1", [T, D])
        d["x2"] = self.dscr("x2", [T, D])
        if self.debug:
            d["dbg"] = self.dscr("dbg", [128, 1024])

        with contextlib.ExitStack() as top:
            self.S = Sched(nc, top)
            S = self.S

            def sbt(name, shape, dt=F32):
                return top.enter_context(nc.sbuf_tensor(name, list(shape), dt))

            self.ps = [top.enter_context(nc.psum_tensor("ps%d" % i, [128, 512], F32)) for i in range(6)]
            self.psb = [top.enter_context(nc.psum_tensor("psb%d" % i, [128, 1024], BF16)) for i in range(2)]
            self.cf = sbt("cf", [128, 512])
            self.cb = sbt("cbf", [128, 384], BF16)
            self.adaT = sbt("adaT", [128, 48])
            self.lall = sbt("lall", [128, 64, 4])
            self.cl = sbt("cl_hm", [128, 4, 64])
            self.incl = sbt("incl_hm", [128, 4, 64])
            self.cmid = sbt("cmid_hm", [128, 4, 64])
            self.ident_f = self.cf[:, 0:128]
            self.ones_f = self.cf[:, 128:256]
            self.U_f = self.cf[:, 256:384]
            self.ident_b = self.cb[:, 0:128]
            self.tri_b = self.cb[:, 256:384]
            self.epsb = self.cf[:, 420:421]

            self.dma(self.cf[:], d["consts"][:, 0:512], [], ["cf"])
            self.dma(self.cb[:], d["consts"][:, 0:384], [], ["cb"], q="gpsimd")
            self.end_phase()
            if "r" in self.phases or "1" in self.phases:
                self.phase_rot()
            for l in range(L):
                xin = d["x"] if l == 0 else d["x2"]
                xout = d["out"] if l == L - 1 else d["x2"]
                if "0" in self.phases:
                    self.phase0(l)
                if "1" in self.phases:
                    self.phase1(l, xin)
                if "2" in self.phases:
                    self.phase2(l)
                if "3" in self.phases:
                    self.phase3(l, xin)
                if "4" in self.phases:
                    self.phase4(l, xout)
            self.end_phase()

    def end_phase(self):
        self.S.barrier()
        self.S.flush()

    def phase_rot(self):
        nc, d = self.nc, self.d
        TWO_PI = float(2 * np.pi)
        PI = float(np.pi)
        with contextlib.ExitStack() as st:
            def sbt(name, shape, dt=F32):
                return st.enter_context(nc.sbuf_tensor(self.uniq(name), list(shape), dt))
            posi = sbt("r_posi", [128, 512], I32)
            ang = sbt("r_ang", [128, 512])
            kf = sbt("r_kf", [128, 512])
            ki = sbt("r_ki", [128, 512], I32)
            r2 = sbt("r_r2", [128, 512])
            sC = [sbt("r_sC%d" % i, [128, 512]) for i in range(2)]
            sS = [sbt("r_sS%d" % i, [128, 512]) for i in range(2)]
            invf = self.cf[:, 384:385]
            sgn = self.cf[:, 385:386]

            def reduce(buf, tag):
                self.ts("vector", kf[:], buf[:], float(1 / TWO_PI), None, ALU.mult, None, [tag], ["kf"])
                self.cp("vector", ki[:], kf[:], ["kf"], ["ki"])
                self.cp("vector", kf[:], ki[:], ["ki"], ["kf"])
                self.stt("vector", buf[:], kf[:], -TWO_PI, buf[:], ALU.mult, ALU.add, ["kf", tag], [tag])
                self.ts("vector", kf[:], buf[:], PI, -TWO_PI, ALU.is_gt, ALU.mult, [tag], ["kf"])
                self.tt("vector", buf[:], buf[:], kf[:], ALU.add, ["kf", tag], [tag])
                self.ts("vector", kf[:], buf[:], -PI, TWO_PI, ALU.is_lt, ALU.mult, [tag], ["kf"])
                self.tt("vector", buf[:], buf[:], kf[:], ALU.add, ["kf", tag], [tag])

            for g in range(NG):
                sl = slice(g * 512, (g + 1) * 512)
                b = g % 2
                self.dma(posi[:], d["pos"][:, sl].partition_broadcast(128), [], ["posi"], sk="posi")
                self.cp("vector", ang[:], posi[:], ["posi"], ["ang"])
                self.ts("vector", ang[:], ang[:], invf, None, ALU.mult, None, ["ang"], ["ang"])
                reduce(ang, "ang")
                self.act(sS[b][:], ang[:], AF.Sin, ["ang"], [("sS", b)])
                self.ts("vector", r2[:], ang[:], PI / 2, None, ALU.add, None, ["ang"], ["r2"])
                reduce(r2, "r2")
                self.act(sC[b][:], r2[:], AF.Sin, ["r2"], [("sC", b)])
                self.ts("gpsimd", sS[b][:], sS[b][:], sgn, None, ALU.mult, None, [("sS", b)], [("sS", b)])
                self.dma(d["rotC"][:, sl], sC[b][:], [("sC", b)], [], sk=("sC", b))
                self.dma(d["rotS"][:, sl], sS[b][:], [("sS", b)], [], sk=("sS", b))
            self.end_phase()

    def phase0(self, l):
        nc, d = self.nc, self.d
        with contextlib.ExitStack() as st:
            def sbt(name, shape, dt=F32):
                return st.enter_context(nc.sbuf_tensor(self.uniq(name), list(shape), dt))
            c_sb = sbt("a_c", [128, 8])
            cond_bc = sbt("a_cond", [128, 8, 128], BF16)
            wa = [sbt("a_w%d" % i, [128, 8, 512], BF16) for i in range(2)]
            bada = sbt("a_b", [128, 6 * D])
            ada = sbt("a_ada", [128, 6 * D])
            tmp = sbt("a_tmp", [128, 16, 128])
            self.dma(c_sb[:], d["c"], [], ["c_sb"])
            self.dma(bada[:], d["b_ada"][l].partition_broadcast(128), [], ["bada"])
            self.act(c_sb[:], c_sb[:], AF.Silu, ["c_sb"], ["c_sb"])
            self.cp("vector", cond_bc[:], c_sb[:].unsqueeze(2).to_broadcast([128, 8, 128]), ["c_sb"], ["cond"])
            for blk in range(12):
                b = blk % 2
                cs = slice(blk * 512, (blk + 1) * 512)
                self.dma(wa[b][:], d["w_ada"][l][:, :, cs], [], [("wa", b)], q="gpsimd", sk=("wa", b))
                pb = self.ps[b]
                for k in range(8):
                    self.mm(pb[:], cond_bc[:, k, :], wa[b][:, k, :], k == 0, k == 7, ["cond", ("wa", b)], [("ps", b)])
                self.tt("vector", ada[:, cs], pb[:], bada[:, cs], ALU.add, [("ps", b), "bada"], [("ada", blk)])
            allada = [("ada", blk) for blk in range(12)]
            for p_ in range(3):
                self.tt("vector", tmp[:], ada[:, p_ * 2048:(p_ + 1) * 2048].rearrange("p (a b) -> p a b", b=128),
                        self.ident_f.unsqueeze(1).to_broadcast([128, 16, 128]), ALU.mult, allada, ["atmp"])
                self.S.op("vector", (lambda o, i: (lambda e: e.tensor_reduce(o, i, AX.X, ALU.add)))(
                    self.adaT[:, p_ * 16:(p_ + 1) * 16], tmp[:]), ["atmp"], ["adaT"])
            for c0 in (8, 32):
                self.ts("vector", self.adaT[:, c0:c0 + 8], self.adaT[:, c0:c0 + 8], 1.0, None, ALU.add, None, ["adaT"], ["adaT"])
            self.dma(d["ada_row"], ada[0:1, :], allada, [], sk="adast")
            self.end_phase()

    def ln_stats(self, src, tag, stats, mv, rstd, nb, tg):
        S = self.S
        for hf in range(2):
            S.op("vector", (lambda o, i: (lambda e: e.bn_stats(o, i)))(stats[:, hf, :], src[:, hf * 512:(hf + 1) * 512]),
                 [tag], [(tg, "st", hf)])
        S.op("vector", (lambda o, i: (lambda e: e.bn_aggr(o, i)))(mv[:], stats[:]), [(tg, "st", 0), (tg, "st", 1)], [(tg, "mv")])
        self.act(rstd[:], mv[:, 1:2], AF.Sqrt, [(tg, "mv")], [(tg, "rstd")], bias=self.epsb)
        S.op("vector", (lambda o: (lambda e: e.reciprocal(o, o)))(rstd[:]), [(tg, "rstd")], [(tg, "rstd")])
        self.ts("vector", nb[:], mv[:, 0:1], -1.0, rstd[:], ALU.mult, ALU.mult, [(tg, "mv"), (tg, "rstd")], [(tg, "nb")])

    def phase1(self, l, xin):
        nc, S, d = self.nc, self.S, self.d
        ps, psb = self.ps, self.psb
        with contextlib.ExitStack() as st:
            def sbt(name, shape, dt=F32):
                return st.enter_context(nc.sbuf_tensor(self.uniq(name), list(shape), dt))
            w1 = sbt("p1_w1", [128, 8, NC1], BF16)
            b1T = sbt("p1_b1T", [128, NCT])
            b1Ts = sbt("p1_b1Ts", [128, NCT])
            b1row = sbt("p1_b1row", [128, 516])
            wpb = sbt("p1_wpb", [128, 2, 128], BF16)
            pscT = sbt("p1_pscT", [128, 2])
            convT = sbt("p1_convT", [128, 2, 3])
            x_sb = [sbt("p1_x%d" % i, [128, 1024]) for i in range(2)]
            xn = [sbt("p1_xn%d" % i, [128, 1024], BF16) for i in range(2)]
            stats = [sbt("p1_st%d" % i, [128, 2, 6]) for i in range(2)]
            mv = [sbt("p1_mv%d" % i, [128, 2]) for i in range(2)]
            rstd = [sbt("p1_rs%d" % i, [128, 1]) for i in range(2)]
            nb = [sbt("p1_nb%d" % i, [128, 1]) for i in range(2)]
            uT = [sbt("p1_uT%d" % i, [128, 8, 512], BF16) for i in range(2)]
            stage = [sbt("p1_stg%d" % i, [128, 512], BF16) for i in range(4)]
            pbuf = [sbt("p1_pb%d" % i, [128, 528]) for i in range(2)]
            sA = sbt("p1_sA", [128, 528])
            sB = sbt("p1_sB", [128, 528])
            pooled = sbt("p1_pooled", [128, 512], BF16)
            t16 = sbt("p1_t16", [128, 16])
            cbt = sbt("p1_cb", [128, 512])
            cct = sbt("p1_cc", [128, 512])
            cht = sbt("p1_ch", [128, 512])
            vbuf = [sbt("p1_vb%d" % i, [128, 514]) for i in range(2)]
            zt = sbt("p1_z", [128, 512])
            rc = sbt("p1_rc", [128, 512])
            rs = sbt("p1_rsn", [128, 512])
            a1 = sbt("p1_a1", [128, 512])
            a2 = sbt("p1_a2", [128, 512])
            swp = sbt("p1_swp", [128, 4, 512])
            zt2 = sbt("p1_z2", [128, 512])
            vst = [sbt("p1_vst%d" % i, [128, 512], BF16) for i in range(2)]
            tot = sbt("p1_tot", [128, 4, 64])
            zer = sbt("p1_zer", [128, 64])
            invw = self.cf[:, 386:388]
            icnt = self.cf[:, 388:420]

            for k in range(8):
                self.dma(w1[:, k, :], d["w1"][l][:, k, :], [], [("w1", k)], q="gpsimd", sk=("wk", k))
            self.dma(wpb[:], d["wpb"][l], [], ["wpb"], q="gpsimd")
            self.dma(b1T[:], d["b1T"][l], [], ["b1T"])
            self.dma(b1row[:], d["b1row"][l].partition_broadcast(128), [], ["b1row"])
            self.dma(pscT[:], d["pscT"][l], [], ["pscT"])
            self.dma(convT[:], d["convT"][l], [], ["convT"])
            self.ts("vector", b1Ts[:], b1T[:], 0.125, None, ALU.mult, None, ["b1T"], ["b1Ts"])
            for t in range(2):
                self.ms("gpsimd", pbuf[t][:, 0:16], 0.0, [], [("pbuf", t)])
                self.ms("gpsimd", vbuf[t][:, 0:2], 0.0, [], [("vbuf", t)])
            self.ms("gpsimd", zer[:], 0.0, [], ["zer"])
            allw1 = [("w1", k) for k in range(8)]
            state = {"ps": 0, "stg": 0}

            def next_ps():
                state["ps"] = (state["ps"] + 1) % 4
                return state["ps"]

            def next_stage():
                state["stg"] = (state["stg"] + 1) % 4
                return state["stg"]

            def group(g):
                ub = g % 2
                gs = slice(g * 512, (g + 1) * 512)
                uTr = [("uT", ub, i) for i in range(4)]
                for i in range(4):
                    tb = i % 2
                    r0 = g * 512 + i * 128
                    self.dma(x_sb[tb][:], xin[r0:r0 + 128, :], [], [("x", tb)], sk=("x", tb))
                    self.ln_stats(x_sb[tb], ("x", tb), stats[tb], mv[tb], rstd[tb], nb[tb], ("ln", tb))
                    self.act(xn[tb][:], x_sb[tb][:], AF.Identity, [("x", tb), (("ln", tb), "rstd"), (("ln", tb), "nb")], [("xn", tb)],
                             bias=nb[tb][:], scale=rstd[tb][:])
                    for c in range(8):
                        self.tr(psb[tb][:, c * 128:(c + 1) * 128], xn[tb][:, c * 128:(c + 1) * 128], [("xn", tb), "cb"], [("psb", tb)])
                    for c in range(8):
                        self.act(uT[ub][:, c, i * 128:(i + 1) * 128], psb[tb][:, c * 128:(c + 1) * 128], AF.Identity,
                                 [("psb", tb), "adaT"], [("uT", ub, i)], bias=self.adaT[:, c:c + 1], scale=self.adaT[:, 8 + c:9 + c])
                self.dma(d["uT"][:, :, gs], uT[ub][:], uTr, [], sk=("uTst", ub))
                self.dma(rc[:], d["rotC"][:, gs], [], ["rc"], sk="rc")
                self.dma(rs[:], d["rotS"][:, gs], [], ["rs"], sk="rs")

                def coltile(j, M=128):
                    b = next_ps()
                    for k in range(8):
                        self.mm(ps[b][0:M, :], w1[:, k, j * 128:j * 128 + M], uT[ub][:, k, :], k == 0, k == 7, uTr + allw1, [("ps", b)])
                    return b

                for w_ in range(4):
                    b = coltile(8 + w_)
                    self.act(swp[:, w_, :], ps[b][:], AF.Identity, [("ps", b), "b1T"], [("swp", w_)], bias=b1T[:, 8 + w_:9 + w_])
                for t in range(2):
                    b = coltile(t)
                    pb_ = pbuf[t]
                    pt = ("pbuf", t)
                    self.act(pb_[:, 16:528], ps[b][:], AF.Identity, [("ps", b), "b1T"], [pt], bias=b1T[:, t:t + 1])
                    self.tt("gpsimd", sA[:, 1:528], pb_[:, 1:528], pb_[:, 0:527], ALU.add, [pt], ["sA"])
                    self.tt("gpsimd", sB[:, 3:528], sA[:, 3:528], sA[:, 1:526], ALU.add, ["sA"], ["sB"])
                    if t == 1:
                        self.tt("gpsimd", sA[:, 7:528], sB[:, 7:528], sB[:, 3:524], ALU.add, ["sB", "sA"], ["sA"])
                        self.tt("gpsimd", sB[:, 15:528], sA[:, 15:528], sA[:, 7:520], ALU.add, ["sA", "sB"], ["sB"])
                    lo, hi = sA, sB
                    if g == 0:
                        for (r0_, src_) in ((0, lo), (64, hi)):
                            self.tt("gpsimd", t16[r0_:r0_ + 64, :], src_[r0_:r0_ + 64, 16:32], icnt[r0_:r0_ + 64, t * 16:(t + 1) * 16], ALU.mult,
                                    ["sA", "sB"], ["t16"])
                    self.ts("gpsimd", lo[0:64, 16:528], lo[0:64, 16:528], invw[0:64, t:t + 1], None, ALU.mult, None, ["sA", "sB", "t16"], ["sA"])
                    self.ts("gpsimd", hi[64:128, 16:528], hi[64:128, 16:528], invw[64:128, t:t + 1], None, ALU.mult, None, ["sA", "sB", "t16"], ["sB"])
                    self.tt("gpsimd", pooled[0:64, :], lo[0:64, 16:528], pb_[0:64, 16:528], ALU.subtract, ["sA", pt], ["pooled_lo"])
                    self.tt("gpsimd", pooled[64:128, :], hi[64:128, 16:528], pb_[64:128, 16:528], ALU.subtract, ["sB", pt], ["pooled_hi"])
                    if g == 0:
                        self.tt("gpsimd", pooled[:, 0:16], t16[:, :], pb_[:, 16:32], ALU.subtract,
                                ["t16", pt, "pooled_lo", "pooled_hi"], ["pooled_lo", "pooled_hi"])
                    self.cp("gpsimd", pb_[:, 0:16], pb_[:, 512:528], [pt, "pooled_lo", "pooled_hi", "sA", "sB"], [pt])
                    b2 = next_ps()
                    self.mm(ps[b2][:], wpb[:, t, :], pooled[:], True, True, ["pooled_lo", "pooled_hi", "wpb"], [("ps", b2)])
                    sg = next_stage()
                    self.ts("vector", stage[sg][:], ps[b2][:], pscT[:, t:t + 1], None, ALU.mult, None, [("ps", b2), "pscT"], [("stage", sg)])
                    self.dma(d["yT"][t][:, gs], stage[sg][:], [("stage", sg)], [], sk=("stg", sg))
                for t in range(2):
                    vb_ = vbuf[t]
                    vt = ("vbuf", t)
                    for (j, dst, tg) in ((2 + t, cbt, "cbt"), (4 + t, cct, "cct"), (6 + t, cht, "cht")):
                        b = coltile(j)
                        self.act(dst[:], ps[b][:], AF.Identity, [("ps", b), "b1T"], [tg], bias=b1T[:, j:j + 1])
                    self.tt("gpsimd", vb_[:, 2:514], cct[:], cht[:], ALU.mult, ["cct", "cht"], [vt])
                    self.ts("gpsimd", zt[:], vb_[:, 2:514], convT[:, t, 2:3], None, ALU.mult, None, [vt, "convT"], ["zt"])
                    self.ts("gpsimd", zt2[:], vb_[:, 1:513], convT[:, t, 1:2], None, ALU.mult, None, [vt, "convT"], ["zt2"])
                    self.tt("gpsimd", zt[:], zt[:], zt2[:], ALU.add, ["zt", "zt2"], ["zt"])
                    self.ts("gpsimd", zt2[:], vb_[:, 0:512], convT[:, t, 0:1], None, ALU.mult, None, [vt, "convT", "zt"], ["zt2"])
                    self.tt("gpsimd", zt[:], zt[:], zt2[:], ALU.add, ["zt", "zt2"], ["zt"])
                    sg = next_stage()
                    self.tt("gpsimd", stage[sg][:], cbt[:], zt[:], ALU.mult, ["cbt", "zt"], [("stage", sg)])
                    self.cp("gpsimd", vb_[:, 0:2], vb_[:, 512:514], [vt], [vt])
                    self.dma(d["yT"][2 + t][:, gs], stage[sg][:], [("stage", sg)], [], sk=("stg", sg))
                for qi in range(8):
                    j = 12 + qi
                    is_q = qi in (0, 1, 4, 5)
                    is_moba = qi < 4
                    sc_ = 0.125 if is_q else 1.0
                    bt = b1Ts if is_q else b1T
                    b = coltile(j)
                    sg = next_stage()
                    self.act(stage[sg][:], ps[b][:], AF.Identity, [("ps", b), "b1T", "b1Ts"], [("stage", sg)], bias=bt[:, j:j + 1], scale=sc_)
                    if is_moba:
                        w_ = qi
                        for hh in range(2):
                            r0_ = hh * 64
                            self.stt("vector", a1[r0_:r0_ + 16, :], ps[b][r0_:r0_ + 16, :], b1T[r0_:r0_ + 16, j:j + 1], rc[r0_:r0_ + 16, :],
                                     ALU.add, ALU.mult, [("ps", b), "b1T", "rc"], ["a1"])
                            self.tt("gpsimd", a2[r0_:r0_ + 16, :], swp[r0_:r0_ + 16, w_, :], rs[r0_:r0_ + 16, :], ALU.mult, [("swp", w_), "rs"], ["a2"])
                            self.tt("gpsimd", a2[r0_:r0_ + 16, :], a2[r0_:r0_ + 16, :], a1[r0_:r0_ + 16, :], ALU.add, ["a1", "a2"], ["a2"])
                            self.ts("gpsimd", stage[sg][r0_:r0_ + 16, :], a2[r0_:r0_ + 16, :], sc_, None, ALU.mult, None, ["a2"], [("stage", sg)])
                    self.dma(d["qk"][qi][:, gs], stage[sg][:], [("stage", sg)], [], sk=("stg", sg))
                bF = next_ps()
                for i in range(4):
                    b = next_ps()
                    if b == bF:
                        b = next_ps()
                    vs_ = i % 2
                    for k in range(8):
                        self.mm(ps[b][:], uT[ub][:, k, i * 128:(i + 1) * 128], w1[:, k, 2560:3072], k == 0, k == 7, uTr + allw1, [("ps", b)])
                    self.tt("vector", vst[vs_][:], ps[b][:], b1row[:, 0:512], ALU.add, [("ps", b), "b1row"], [("vst", vs_)])
                    r0 = g * 512 + i * 128
                    self.dma(d["v"][r0:r0 + 128, :], vst[vs_][:], [("vst", vs_)], [], sk=("vst", vs_))
                    for k in range(8):
                        self.mm(ps[bF][:, i * 4:(i + 1) * 4], uT[ub][:, k, i * 128:(i + 1) * 128], w1[:, k, 3072:3076], k == 0, k == 7,
                                uTr + allw1, [("ps", bF)])
                self.tt("vector", self.lall[:, g * 4:(g + 1) * 4, :], ps[bF][:, 0:16].rearrange("p (a b) -> p a b", b=4),
                        b1row[:, 512:516].unsqueeze(1).to_broadcast([128, 4, 4]), ALU.add, [("ps", bF), "b1row"], [("lall", g)])

            for g in range(NG):
                group(g)
            lr = [("lall", g) for g in range(NG)]
            lflat = self.lall[:].rearrange("p a b -> p (a b)")
            self.act(lflat, lflat, AF.Exp, lr, ["lall2"], scale=-1.0)
            self.act(lflat, lflat, AF.Ln, ["lall2"], ["lall3"], bias=1.0)
            self.mm(ps[0][:, 0:256], self.U_f, lflat, True, True, ["lall3"], [("ps", 0)])
            self.mm(ps[1][:, 0:256], self.ones_f, lflat, True, True, ["lall3"], [("ps", 1)])
            self.cp("vector", tot[:], ps[1][:, 0:256].rearrange("p (t h) -> p h t", h=4), [("ps", 1)], ["tot"])
            for h in range(4):
                S.op("vector", (lambda o, a, z: (lambda e: e.tensor_tensor_scan(o, a, z, 0.0, ALU.add, ALU.add)))(
                    self.incl[:, h, :], tot[:, h, :], zer[:]), ["tot", "zer"], [("incl", h)])
            inclr = [("incl", h) for h in range(4)]
            self.tt("vector", tot[:], self.incl[:], tot[:], ALU.subtract, inclr + ["tot"], ["tot"])
            self.tt("vector", self.cl[:], ps[0][:, 0:256].rearrange("p (t h) -> p h t", h=4), tot[:], ALU.add, [("ps", 0), "tot"], ["cl"])
            self.tt("vector", self.cmid[:], tot[:], self.incl[:], ALU.add, inclr + ["tot"], ["cmid"])
            self.ts("vector", self.cmid[:], self.cmid[:], 0.5, None, ALU.mult, None, ["cmid"], ["cmid"])
            if self.debug:
                self.dma(d["dbg"][:, 0:256], self.cl[:].rearrange("p a b -> p (a b)"), ["cl"], [], sk="dbg1")
                self.dma(d["dbg"][:, 256:512], self.incl[:].rearrange("p a b -> p (a b)"), inclr, [], sk="dbg2")
            self.end_phase()

    def phase2(self, l):
        nc, S, d = self.nc, self.S, self.d
        ps = self.ps
        with contextlib.ExitStack() as st:
            def sbt(name, shape, dt=F32):
                return st.enter_context(nc.sbuf_tensor(self.uniq(name), list(shape), dt))
            kT = [sbt("p2_kT%d" % i, [128, T], BF16) for i in range(2)]
            V = [sbt("p2_V%d" % i, [128, NTL, 65], BF16) for i in range(2)]
            qT = [sbt("p2_qT%d" % i, [128, 512], BF16) for i in range(2)]
            pT = [sbt("p2_pT%d" % i, [128, 512], BF16) for i in range(4)]
            o_sb = sbt("p2_o", [64, 512])
            rcr = sbt("p2_rc", [128, 512])
            y_sb = [sbt("p2_y%d" % i, [64, 512], BF16) for i in range(2)]
            sc_sb = [sbt("p2_sc%d" % i, [128, 4, 32]) for i in range(2)]
            m8 = [sbt("p2_m8%d" % i, [128, 4, 8]) for i in range(2)]
            mk = [sbt("p2_mk%d" % i, [128, 4, 96], BF16) for i in range(2)]
            kbf = sbt("p2_kbf", [64, 32])
            kbarT = [sbt("p2_kb%d" % i, [64, 32], BF16) for i in range(2)]
            biasT = [sbt("p2_bias%d" % i, [128, 64]) for i in range(2)]
            psS = [ps[0], ps[1], ps[2]]
            psO = [ps[3], ps[4]]
            psX = ps[5]
            psM = self.psb[0]

            for i in range(2):
                self.dma(kT[i][64:96, :], d["onehot"], [], [("kToh", i)], q="gpsimd")
            for i in range(2):
                self.ms("gpsimd", V[i][:, :, 64:65], 1.0, [], [("Vone", i)])
                self.ms("gpsimd", mk[i][:], 0.0, [], [("mk", i)])
                self.ms("gpsimd", kT[i][96:97, :], 1.0, [], [("kToh", i)])

            heads = [(ty, h) for ty in range(2) for h in range(4)]
            qcount = [0]

            def load_head(hi):
                ty, h = heads[hi]
                bi = hi % 2
                ktile = (2 if ty == 0 else 6) + h // 2
                r0 = (h % 2) * 64
                self.dma(kT[bi][0:64, :], d["qk"][ktile][r0:r0 + 64, :], [], [("kT", bi)], sk=("kT", bi))
                c0 = ty * 256 + h * 64
                self.dma(V[bi][:, :, 0:64], d["v"][:, c0:c0 + 64].rearrange("(n p) c -> p n c", p=128), [], [("V", bi)], sk=("V", bi))

            def load_q(hi, g):
                ty, h = heads[hi]
                qb = qcount[0] % 2
                qcount[0] += 1
                qtile = (0 if ty == 0 else 4) + h // 2
                r0 = (h % 2) * 64
                self.dma(qT[qb][0:64, :], d["qk"][qtile][r0:r0 + 64, g * 512:(g + 1) * 512], [], [("qT", qb)], sk=("qT", qb))
                return qb

            def prep(hi, g, qb):
                ty, h = heads[hi]
                bi = hi % 2
                if ty == 0:
                    self.ms("gpsimd", sc_sb[qb][:], -1e30, [], [("sc", qb)])
                    for i in range(4):
                        self.mm(psX[:, i * 32:(i + 1) * 32], qT[qb][0:64, i * 128:(i + 1) * 128], kbarT[bi][:, :], True, True,
                                [("qT", qb), ("kbar", bi)], ["psX"])
                    for i in range(4):
                        own = (4 * g + i) // 2
                        if own > 0:
                            self.cp("vector", sc_sb[qb][:, i, 0:own], psX[:, i * 32:i * 32 + own], ["psX", ("sc", qb)], [("sc", qb)])
                    for i in range(4):
                        S.op("vector", (lambda o, a: (lambda e: e.max(o, a)))(m8[qb][:, i, :], sc_sb[qb][:, i, :]), [("sc", qb)], [("m8", qb)])
                    for i in range(4):
                        self.ts("vector", mk[qb][:, i, 64:96], sc_sb[qb][:, i, :], m8[qb][:, i, 2:3], NEG, ALU.is_lt, ALU.mult,
                                [("sc", qb), ("m8", qb)], [("mk", qb)])
                    for i in range(4):
                        own = (4 * g + i) // 2
                        self.ms("gpsimd", mk[qb][:, i, 64 + own:65 + own], 0.0, [("mk", qb)], [("mk", qb)])
                    for i in range(4):
                        self.tr(psM[0:96, i * 128:(i + 1) * 128], mk[qb][:, i, 0:96], [("mk", qb), "cb"], ["psM"])
                    self.act(qT[qb][64:96, :], psM[64:96, 0:512], AF.Copy, ["psM", ("qT", qb)], [("qT", qb)])
                else:
                    n = 4 * g + 4
                    self.ts("vector", biasT[qb][:, 0:n], self.cl[:, h, 0:n], self.incl[:, h, 4 * g + 1:4 * g + 2], None, ALU.subtract, None,
                            [], [("biasT", qb)])
                    if h == 0 and g < 2:
                        self.ms("gpsimd", qT[qb][64:96, :], 0.0, [("qT", qb)], [("qT", qb)])
                    self.ts("vector", qT[qb][96:97, :].rearrange("p (a b) -> p a b", b=128),
                            self.cmid[96:97, h, 4 * g:4 * g + 4].unsqueeze(2).to_broadcast([1, 4, 128]),
                            -1.0, self.incl[96:97, h, 4 * g + 1:4 * g + 2], ALU.mult, ALU.add, [("qT", qb)], [("qT", qb)])

            def main(hi, g, qb):
                ty, h = heads[hi]
                bi = hi % 2
                K = 96 if ty == 0 else 97
                nkt = 4 * g + 4
                ob = g % 2

                def emit_S(kt):
                    nq0 = max(0, kt - 4 * g) * 128
                    sb_ = kt % 3
                    pb_ = kt % 4
                    self.mm(psS[sb_][:, nq0:512], kT[bi][0:K, kt * 128:(kt + 1) * 128], qT[qb][0:K, nq0:512], True, True,
                            [("kT", bi), ("kToh", bi), ("qT", qb)], [("psS", sb_)])
                    if ty == 0:
                        self.act(pT[pb_][:, nq0:512], psS[sb_][:, nq0:512], AF.Exp, [("psS", sb_)], [("pT", pb_)])
                    else:
                        self.act(pT[pb_][:, nq0:512], psS[sb_][:, nq0:512], AF.Exp, [("psS", sb_), ("biasT", qb)], [("pT", pb_)],
                                 bias=biasT[qb][:, kt:kt + 1])
                    if kt >= 4 * g:
                        self.tt("gpsimd", pT[pb_][:, nq0:nq0 + 128], pT[pb_][:, nq0:nq0 + 128], self.tri_b, ALU.mult, [("pT", pb_)], [("pT", pb_)])

                def emit_PV(kt):
                    nq0 = max(0, kt - 4 * g) * 128
                    pb_ = kt % 4
                    self.mm(psO[ob][0:65, nq0:512], V[bi][:, kt, 0:65], pT[pb_][:, nq0:512], kt == 0, kt == nkt - 1,
                            [("pT", pb_), ("V", bi), ("Vone", bi)], [("psO", ob)])

                SK = 2
                for kt in range(nkt + SK):
                    if kt < nkt:
                        emit_S(kt)
                    if kt >= SK:
                        emit_PV(kt - SK)
                self.act(o_sb[:], psO[ob][0:64, :], AF.Copy, [("psO", ob)], ["o_sb"])
                S.op("vector", (lambda o, a: (lambda e: e.reciprocal(o, a)))(rcr[64:65, :], psO[ob][64:65, :]), [("psO", ob)], ["rcr"])
                self.mm(psX[0:64, :], self.ones_f[64:65, 0:64], rcr[64:65, :], True, True, ["rcr"], ["psX"])
                yb = g % 2
                self.tt("vector", y_sb[yb][:], o_sb[:], psX[0:64, :], ALU.mult, ["o_sb", "psX"], [("y", yb)])
                ytile = 4 + ty * 2 + h // 2
                r0 = (h % 2) * 64
                self.dma(d["yT"][ytile][r0:r0 + 64, g * 512:(g + 1) * 512], y_sb[yb][:], [("y", yb)], [], sk=("y", yb))

            load_head(0)
            for hi in range(8):
                ty, h = heads[hi]
                bi = hi % 2
                if hi + 1 < 8:
                    load_head(hi + 1)
                if ty == 0:
                    S.op("vector", (lambda o, a: (lambda e: e.tensor_reduce(o, a, AX.X, ALU.add)))(
                        kbf[:], kT[bi][0:64, :].rearrange("p (n c) -> p n c", c=256)), [("kT", bi)], ["kbf"])
                    self.ts("vector", kbarT[bi][:], kbf[:], 1.0 / 256.0, None, ALU.mult, None, ["kbf"], [("kbar", bi)])
                qb = load_q(hi, 0)
                prep(hi, 0, qb)
                for g in range(NG):
                    qb_n = None
                    if g + 1 < NG:
                        qb_n = load_q(hi, g + 1)
                        prep(hi, g + 1, qb_n)
                    main(hi, g, qb)
                    qb = qb_n
            self.end_phase()

    def residual_ln(self, pfx, pA, pB, tagA, tagB, xt, tagx, gbc, lng, lnb, tmp, stats, mv, rstd, nb, dst_ap):
        t = (pfx, "tmp")
        self.tt("vector", tmp[:, 0:512], pA[:], gbc[:, 0:512], ALU.mult, [tagA, (pfx, "gbc")], [(pfx, "tmpa")])
        self.tt("vector", tmp[:, 512:1024], pB[:], gbc[:, 512:1024], ALU.mult, [tagB, (pfx, "gbc")], [(pfx, "tmpb")])
        self.ts("gpsimd", xt[:], xt[:], float(ALPHA), None, ALU.mult, None, [tagx], [tagx])
        self.tt("gpsimd", tmp[:], tmp[:], xt[:], ALU.add, [(pfx, "tmpa"), (pfx, "tmpb"), tagx], [t])
        self.ln_stats(tmp, t, stats, mv, rstd, nb, (pfx, "ln"))
        self.act(tmp[:], tmp[:], AF.Identity, [t, ((pfx, "ln"), "rstd"), ((pfx, "ln"), "nb")], [t], bias=nb[:], scale=rstd[:])
        self.tt("gpsimd", tmp[:], tmp[:], lng[:], ALU.mult, [t, (pfx, "lng")], [t])
        self.tt("gpsimd", tmp[:], tmp[:], lnb[:], ALU.add, [t, (pfx, "lnb")], [t])
        self.dma(dst_ap, tmp[:], [t], [], sk=("xst", pfx))

    def phase3(self, l, xin):
        nc, S, d = self.nc, self.S, self.d
        ps = self.ps
        with contextlib.ExitStack() as st:
            def sbt(name, shape, dt=F32):
                return st.enter_context(nc.sbuf_tensor(self.uniq(name), list(shape), dt))
            w2 = sbt("p3_w2", [128, 8, 4096], BF16)
            wbr = sbt("p3_wbr", [128, 8, D], BF16)
            wo = sbt("p3_wo", [128, 8, D], BF16)
            b2T = sbt("p3_b2T", [128, 32])
            g1 = sbt("p3_g1", [128, D])
            lng = sbt("p3_lng", [128, D])
            lnb = sbt("p3_lnb", [128, D])
            uT = sbt("p3_uT", [128, 8, 512], BF16)
            yT = sbt("p3_yT", [128, 8, 512], BF16)
            xt = [sbt("p3_xt%d" % i, [128, D]) for i in range(2)]
            mT = sbt("p3_mT", [128, 8, 512], BF16)
            gate = [sbt("p3_gate%d" % i, [128, 512]) for i in range(2)]
            term = [sbt("p3_term%d" % i, [128, 512]) for i in range(2)]
            acc = sbt("p3_acc", [128, 512])
            tmp = [sbt("p3_tmp%d" % i, [128, D]) for i in range(2)]
            stats = [sbt("p3_st%d" % i, [128, 2, 6]) for i in range(2)]
            mv = [sbt("p3_mv%d" % i, [128, 2]) for i in range(2)]
            rstd = [sbt("p3_rs%d" % i, [128, 1]) for i in range(2)]
            nb = [sbt("p3_nb%d" % i, [128, 1]) for i in range(2)]
            for k in range(8):
                self.dma(w2[:, k, :], d["w2"][l][:, k, :], [], [("w2", k)], q="gpsimd", sk=("wk", k))
            self.dma(wbr[:], d["wbr"][l], [], ["wbr"], q="gpsimd")
            self.dma(wo[:], d["wo"][l], [], ["wo"], q="gpsimd")
            self.dma(b2T[:], d["b2T"][l], [], ["b2T"])
            self.dma(g1[:], d["ada_row"][:, 2048:3072].partition_broadcast(128), [], [("p3a", "gbc"), ("p3b", "gbc")])
            self.dma(lng[:], d["lnrow"][l][0:1, :].partition_broadcast(128), [], [("p3a", "lng"), ("p3b", "lng")])
            self.dma(lnb[:], d["lnrow"][l][1:2, :].partition_broadcast(128), [], [("p3a", "lnb"), ("p3b", "lnb")])
            allw2 = [("w2", k) for k in range(8)]
            cnt = [0]

            def group(g):
                gs = slice(g * 512, (g + 1) * 512)
                self.dma(uT[:], d["uT"][:, :, gs], [], ["uT"], sk="p3uT")
                self.dma(yT[:], d["yT"][:, :, gs].rearrange("n p t -> p n t"), [], ["yT"], sk="p3yT")
                for dc in range(8):
                    for n in range(4):
                        c_ = cnt[0] % 2
                        cnt[0] += 1
                        pg = ps[c_]
                        pbr = ps[2 + c_]
                        for k in range(8):
                            self.mm(pg[:], w2[:, k, n * 1024 + dc * 128:n * 1024 + (dc + 1) * 128], uT[:, k, :], k == 0, k == 7,
                                    ["uT"] + allw2, [("psg", c_)])
                        for c in range(2):
                            self.mm(pbr[:], wbr[:, n * 2 + c, dc * 128:(dc + 1) * 128], yT[:, n * 2 + c, :], c == 0, c == 1,
                                    ["yT", "wbr"], [("psbr", c_)])
                        self.act(gate[c_][:], pg[:], AF.Sigmoid, [("psg", c_), "b2T"], [("gate", c_)], bias=b2T[:, n * 8 + dc:n * 8 + dc + 1])
                        if n == 0:
                            self.tt("vector", acc[:], gate[c_][:], pbr[:], ALU.mult, [("gate", c_), ("psbr", c_)], ["acc"])
                        else:
                            self.tt("vector", term[c_][:], gate[c_][:], pbr[:], ALU.mult, [("gate", c_), ("psbr", c_)], [("term", c_)])
                            if n < 3:
                                self.tt("gpsimd", acc[:], acc[:], term[c_][:], ALU.add, ["acc", ("term", c_)], ["acc"])
                            else:
                                self.tt("gpsimd", mT[:, dc, :], acc[:], term[c_][:], ALU.add, ["acc", ("term", c_)], [("mT", dc)])
                mTr = [("mT", dc) for dc in range(8)]
                for i in range(4):
                    tb = i % 2
                    pfx = "p3a" if tb == 0 else "p3b"
                    r0 = g * 512 + i * 128
                    self.dma(xt[tb][:], xin[r0:r0 + 128, :], [], [("p3x", tb)], sk=("p3x", tb))
                    pA, pB = ps[4], ps[5]
                    for hf, pp, tg in ((0, pA, "p3psA"), (1, pB, "p3psB")):
                        for k in range(8):
                            self.mm(pp[:], mT[:, k, i * 128:(i + 1) * 128], wo[:, k, hf * 512:(hf + 1) * 512], k == 0, k == 7,
                                    mTr + ["wo"], [tg])
                    self.residual_ln(pfx, pA, pB, "p3psA", "p3psB", xt[tb], ("p3x", tb), g1, lng, lnb, tmp[tb], stats[tb], mv[tb],
                                     rstd[tb], nb[tb], d["x1"][r0:r0 + 128, :])

            for g in range(NG):
                group(g)
            self.end_phase()

    def phase4(self, l, xout):
        nc, S, d = self.nc, self.S, self.d
        ps, psb = self.ps, self.psb
        with contextlib.ExitStack() as st:
            def sbt(name, shape, dt=F32):
                return st.enter_context(nc.sbuf_tensor(self.uniq(name), list(shape), dt))
            wup = sbt("p4_wup", [128, 8, DFF], BF16)
            wdn = sbt("p4_wdn", [128, 32, D], BF16)
            bupT = sbt("p4_bupT", [128, 32])
            g2 = sbt("p4_g2", [128, D])
            lng = sbt("p4_lng", [128, D])
            lnb = sbt("p4_lnb", [128, D])
            u2T = sbt("p4_u2T", [128, 8, 512], BF16)
            hT = sbt("p4_hT", [128, 32, 512], BF16)
            xt = sbt("p4_xt", [128, D])
            xn = sbt("p4_xn", [128, D], BF16)
            rt = sbt("p4_rt", [128, 512])
            tmp = sbt("p4_tmp", [128, D])
            stats = sbt("p4_st", [128, 2, 6])
            mv = sbt("p4_mv", [128, 2])
            rstd = sbt("p4_rs", [128, 1])
            nb = sbt("p4_nb", [128, 1])
            for k in range(8):
                self.dma(wup[:, k, :], d["wup"][l][:, k, :], [], [("wup", k)], q="gpsimd", sk=("wk", k))
            for k in range(0, 32, 4):
                self.dma(wdn[:, k:k + 4, :], d["wdn"][l][:, k:k + 4, :], [], [("wdn", k)], q="gpsimd", sk=("wk2", k))
            self.dma(bupT[:], d["bupT"][l], [], ["bupT"])
            self.dma(g2[:], d["ada_row"][:, 5120:6144].partition_broadcast(128), [], [("p4", "gbc")])
            self.dma(lng[:], d["lnrow"][l][2:3, :].partition_broadcast(128), [], [("p4", "lng")])
            self.dma(lnb[:], d["lnrow"][l][3:4, :].partition_broadcast(128), [], [("p4", "lnb")])
            allwup = [("wup", k) for k in range(8)]
            allwdn = [("wdn", k) for k in range(0, 32, 4)]

            def group(g):
                for i in range(4):
                    tb = i % 2
                    r0 = g * 512 + i * 128
                    self.dma(xt[:], d["x1"][r0:r0 + 128, :], [], ["p4x"], sk="p4x")
                    self.ln_stats(xt, "p4x", stats, mv, rstd, nb, ("p4", "ln"))
                    self.act(xn[:], xt[:], AF.Identity, ["p4x", (("p4", "ln"), "rstd"), (("p4", "ln"), "nb")], ["p4xn"],
                             bias=nb[:], scale=rstd[:])
                    for c in range(8):
                        self.tr(psb[tb][:, c * 128:(c + 1) * 128], xn[:, c * 128:(c + 1) * 128], ["p4xn", "cb"], [("psb", tb)])
                    for c in range(8):
                        self.act(u2T[:, c, i * 128:(i + 1) * 128], psb[tb][:, c * 128:(c + 1) * 128], AF.Identity,
                                 [("psb", tb), "adaT"], [("u2T", i)], bias=self.adaT[:, 24 + c:25 + c], scale=self.adaT[:, 32 + c:33 + c])
                u2r = [("u2T", i) for i in range(4)]
                for fc in range(32):
                    c_ = fc % 2
                    pu = ps[c_]
                    for k in range(8):
                        self.mm(pu[:], wup[:, k, fc * 128:(fc + 1) * 128], u2T[:, k, :], k == 0, k == 7, u2r + allwup, [("psu", c_)])
                    self.ts("vector", rt[:], pu[:], bupT[:, fc:fc + 1], 0.0, ALU.add, ALU.max, [("psu", c_), "bupT"], ["rt"])
                    self.act(hT[:, fc, :], rt[:], AF.Square, ["rt"], [("hT", fc)])
                hr = [("hT", fc) for fc in range(32)]
                for i in range(4):
                    r0 = g * 512 + i * 128
                    par = i % 2
                    pA, pB = ps[2 + par * 2], ps[3 + par * 2]
                    tgA, tgB = ("p4psA", par), ("p4psB", par)
                    for hf, pp, tg in ((0, pA, tgA), (1, pB, tgB)):
                        for fc in range(32):
                            self.mm(pp[:], hT[:, fc, i * 128:(i + 1) * 128], wdn[:, fc, hf * 512:(hf + 1) * 512], fc == 0, fc == 31,
                                    hr + allwdn, [tg])
                    self.dma(xt[:], d["x1"][r0:r0 + 128, :], [], ["p4x"], sk="p4x")
                    self.residual_ln("p4", pA, pB, tgA, tgB, xt, "p4x", g2, lng, lnb, tmp, stats, mv, rstd, nb, xout[r0:r0 + 128, :])

            for g in range(NG):
                group(g)
            self.end_phase()


def prep_weights(w_ada, b_ada, w_in, b_in, w_pool, pool_scale, conv_w, w_branch, w_o, ln1_g, ln1_b,
                 w_up, b_up, w_down, ln2_g, ln2_b, depth=DEPTH):
    L = depth
    f = lambda a: np.ascontiguousarray(np.asarray(a, dtype=np.float32))
    w_in = f(w_in)
    b_in = f(b_in)
    cols = _cols1()
    m = {}
    m["w_ada"] = f(f(w_ada)[:L].reshape(L, 8, 128, 6 * D).transpose(0, 2, 1, 3))
    m["b_ada"] = f(f(b_ada)[:L].reshape(L, 1, 6 * D))
    m["w1"] = f(w_in[:L][:, :, cols].reshape(L, 8, 128, NC1).transpose(0, 2, 1, 3))
    m["b1T"] = f(b_in[:L][:, cols[:NCT * 128]].reshape(L, NCT, 128).transpose(0, 2, 1))
    m["b1row"] = f(b_in[:L][:, cols[NCT * 128:]].reshape(L, 1, 516))
    m["w2"] = f(w_in[:L][:, :, 2564:].reshape(L, 8, 128, 4096).transpose(0, 2, 1, 3))
    m["b2T"] = f(b_in[:L][:, 2564:].reshape(L, 32, 128).transpose(0, 2, 1))
    wp = f(w_pool)[:L]
    wpb = np.zeros((L, 128, 2, 128), np.float32)
    for t in range(2):
        wpb[:, 0:64, t, 0:64] = wp[:, 2 * t]
        wpb[:, 64:128, t, 64:128] = wp[:, 2 * t + 1]
    m["wpb"] = wpb
    m["pscT"] = f(f(pool_scale)[:L].reshape(L, 2, 128).transpose(0, 2, 1))
    m["convT"] = f(f(conv_w)[:L].reshape(L, 3, 2, 128).transpose(0, 3, 2, 1))
    m["wbr"] = f(f(w_branch)[:L].reshape(L, 4, 2, 128, D).transpose(0, 3, 1, 2, 4).reshape(L, 128, 8, D))
    m["wo"] = f(f(w_o)[:L].reshape(L, 8, 128, D).transpose(0, 2, 1, 3))
    m["lnrow"] = f(np.stack([f(ln1_g)[:L], f(ln1_b)[:L], f(ln2_g)[:L], f(ln2_b)[:L]], axis=1))
    m["wup"] = f(f(w_up)[:L].reshape(L, 8, 128, DFF).transpose(0, 2, 1, 3))
    m["bupT"] = f(f(b_up)[:L].reshape(L, 32, 128).transpose(0, 2, 1))
    m["wdn"] = f(f(w_down)[:L].reshape(L, 32, 128, D).transpose(0, 2, 1, 3))
    m["consts"] = _consts()
    m["onehot"] = _onehot()
    return m


_PROG_CACHE = {}


def get_prog(depth=DEPTH, debug=False, phases="01234"):
    key = (depth, debug, phases)
    if key not in _PROG_CACHE:
        _PROG_CACHE[key] = Prog(depth, debug, phases)
    return _PROG_CACHE[key]


def kernel(x, c, positions, w_ada, b_ada, w_in, b_in, w_pool, pool_scale, conv_w, w_branch, w_o,
           ln1_g, ln1_b, w_up, b_up, w_down, ln2_g, ln2_b):
    prog = get_prog()
    wm = prep_weights(w_ada, b_ada, w_in, b_in, w_pool, pool_scale, conv_w, w_branch, w_o, ln1_g, ln1_b,
                      w_up, b_up, w_down, ln2_g, ln2_b)
    x = np.asarray(x, dtype=np.float32)
    c = np.asarray(c, dtype=np.float32)
    positions = np.asarray(positions, dtype=np.int32)
    in_maps = []
    for b in range(BATCH):
        m = dict(wm)
        m["x"] = np.ascontiguousarray(x[b])
        m["c"] = np.ascontiguousarray(c[b].reshape(8, 128).T)
        m["pos"] = np.ascontiguousarray(positions[b].reshape(1, T))
        in_maps.append(m)
    res = run_bass_kernel_spmd(prog.nc, in_maps, core_ids=list(range(BATCH)))
    out = np.stack([np.asarray(res.results[b]["out"], dtype=np.float32) for b in range(BATCH)], axis=0)
    return out
```

```python
import contextlib
import math
import numpy as np
import concourse.bass as bass
import concourse.mybir as mybir
from concourse.bass_utils import run_bass_kernel_spmd

F32 = mybir.dt.float32
BF16 = mybir.dt.bfloat16
I32 = mybir.dt.int32
AF = mybir.ActivationFunctionType
ALU = mybir.AluOpType
AX = mybir.AxisListType

D = 1024
SEQ = 8192
BATCH = 4
DEPTH = 4
DFF = 4096
HD = 64
IN_WIDTH = 6660
ALPHA = (2 * DEPTH) ** 0.25
LN_EPS = 1e-5
ROPE_THETA = 500000.0
T = SEQ
G = 512
NG = T // G
NTL = T // 128
NC1 = 20 * 128 + 512 + 4
NCT = 20
NEG = -30000.0
ENGINES = ("tensor", "vector", "scalar", "gpsimd", "sync")


class Op:
    __slots__ = ("eng", "fn", "deps", "is_dma", "semkey", "ms", "needed")

    def __init__(self, eng, fn, is_dma, semkey):
        self.eng = eng
        self.fn = fn
        self.deps = []
        self.is_dma = is_dma
        self.semkey = semkey
        self.ms = None
        self.needed = False


class Sched:
    def __init__(self, nc, stack):
        self.nc = nc
        self.stack = stack
        self.pending = {e: [] for e in ENGINES}
        self.last_w = {}
        self.readers = {}
        self.sem_count = {}
        self.sems = {}
        self.last_by_key = {}
        self.waited = {e: {} for e in ENGINES}
        self.n_inst = 0
        self.group_keys = set()
        self.trace = {}
        import os
        for i in range(int(os.environ.get("SEM_SKIP", "0"))):
            self.stack.enter_context(nc.semaphore("dummy%d" % i))

    def op(self, eng, fn, reads=(), writes=(), dma=False, semkey=None):
        semkey = (semkey or ("dma", eng)) if dma else ("eng", eng)
        o = Op(eng, fn, dma, semkey)
        deps = set()

        def same_stream(w):
            return (not dma) and (not w.is_dma) and w.eng == eng

        for r in reads:
            w = self.last_w.get(r)
            if w is not None:
                if same_stream(w) and eng == "tensor":
                    continue
                deps.add(w)
        for r in writes:
            w = self.last_w.get(r)
            if w is not None and not same_stream(w):
                deps.add(w)
            for rd in self.readers.get(r, ()):
                if not same_stream(rd):
                    deps.add(rd)
        for r in reads:
            self.readers.setdefault(r, []).append(o)
        for r in writes:
            self.last_w[r] = o
            self.readers[r] = []
        deps.discard(o)
        o.deps = list(deps)
        for d in o.deps:
            d.needed = True
        self.pending[eng].append(o)
        self.last_by_key[semkey] = o
        return o

    def barrier(self):
        targets = list(self.last_by_key.values())
        for t in targets:
            t.needed = True
        for e in ENGINES:
            b = Op(e, None, False, ("eng", e))
            b.deps = targets
            self.pending[e].append(b)
        self.last_w = {}
        self.readers = {}

    def flush(self):
        nc = self.nc
        for e in ENGINES:
            for o in self.pending[e]:
                if o.fn is None:
                    continue
                if o.is_dma:
                    self.sem_count[o.semkey] = self.sem_count.get(o.semkey, 0) + 16
                    o.ms = self.sem_count[o.semkey]
                elif o.needed:
                    self.sem_count[o.semkey] = self.sem_count.get(o.semkey, 0) + 1
                    o.ms = self.sem_count[o.semkey]
        for gk in self.group_keys:
            grp = [o for e in ENGINES for o in self.pending[e] if o.fn is not None and o.semkey == gk]
            if grp:
                mx = max(o.ms for o in grp)
                for o in grp:
                    o.ms = mx
        for k in self.sem_count:
            if k not in self.sems:
                name = "s%d_" % len(self.sems) + "".join(ch if ch.isalnum() else "_" for ch in str(k))[:40]
                self.sems[k] = self.stack.enter_context(nc.semaphore(name))
        with nc.Block() as block:
            for e in ENGINES:
                if self.pending[e]:
                    getattr(block, e)(self._mk_body(e, self.pending[e]))
        self.pending = {e: [] for e in ENGINES}

    def _mk_body(self, e, ops):
        sems = self.sems
        waited = self.waited[e]

        tr = self.trace.setdefault(e, [])

        def body(eng):
            for o in ops:
                tr.append(o)
                need = {}
                for d in o.deps:
                    if d.ms is None:
                        continue
                    if need.get(d.semkey, 0) < d.ms:
                        need[d.semkey] = d.ms
                for k, v in need.items():
                    if waited.get(k, 0) >= v:
                        continue
                    eng.wait_ge(sems[k], v)
                    waited[k] = v
                    self.n_inst += 1
                if o.fn is None:
                    continue
                ins = o.fn(eng)
                self.n_inst += 1
                if o.is_dma:
                    ins.then_inc(sems[o.semkey], 16)
                elif o.ms is not None:
                    ins.then_inc(sems[o.semkey], 1)
        return body


def _cols1():
    cols = []
    cols += list(range(0, 256))
    cols += list(range(256, 1024))
    for base in (1024, 1280):
        for pair in range(2):
            for r in range(128):
                hh, dd = r // 64, r % 64
                src = ((dd + 8) % 16) if dd < 16 else dd
                cols.append(base + (pair * 2 + hh) * 64 + src)
    cols += list(range(1024, 1280))
    cols += list(range(1280, 1536))
    cols += list(range(1792, 2048))
    cols += list(range(2048, 2304))
    cols += list(range(1536, 1792))
    cols += list(range(2304, 2560))
    cols += list(range(2560, 2564))
    assert len(cols) == NC1
    return np.asarray(cols, dtype=np.int64)


def _consts():
    c = np.zeros((128, 1024), np.float32)
    p = np.arange(128)
    c[:, 0:128] = np.eye(128, dtype=np.float32)
    c[:, 128:256] = 1.0
    c[:, 256:384] = (p[:, None] <= p[None, :]).astype(np.float32)
    d = p % 64
    inv = ROPE_THETA ** (-(2.0 * (d % 8)) / 16.0)
    c[:, 384] = np.where(d < 16, inv, 0.0)
    c[:, 385] = np.where(d < 8, -1.0, np.where(d < 16, 1.0, 0.0))
    w_lo = np.array([2.0, 8.0])
    w_hi = np.array([4.0, 16.0])
    for t in range(2):
        w = np.where(p < 64, w_lo[t], w_hi[t])
        c[:, 386 + t] = 1.0 / w
        for j in range(16):
            c[:, 388 + t * 16 + j] = 1.0 / np.minimum(j + 1.0, w)
    c[:, 420] = LN_EPS
    return c


def _onehot():
    oh = np.zeros((32, T), np.float32)
    for n in range(32):
        oh[n, n * 256:(n + 1) * 256] = 1.0
    return oh


class Prog:
    def __init__(self, depth=DEPTH, debug=False, phases="01234"):
        self.depth = depth
        self.debug = debug
        self.phases = phases
        self.nc = bass.Bass("TRN2", target_bir_lowering=False)
        self.build()

    def din(self, name, shape, dt=F32):
        return self.nc.dram_tensor(name, list(shape), dt, kind="ExternalInput").ap()

    def dscr(self, name, shape, dt=F32):
        if self.debug:
            return self.nc.dram_tensor(name, list(shape), dt, kind="ExternalOutput").ap()
        return self.nc.dram_tensor(name, list(shape), dt).ap()

    def uniq(self, name):
        self._uid = getattr(self, "_uid", 0) + 1
        return "%s_u%d" % (name, self._uid)

    def mm(self, out, lhsT, rhs, start, stop, reads, writes):
        self.S.op("tensor", lambda e: e.matmul(out, lhsT, rhs, start=start, stop=stop), reads, writes)

    def tr(self, out, in_, reads, writes):
        ident = self.ident_b
        self.S.op("tensor", lambda e: e.transpose(out, in_, ident), reads, writes)

    def act(self, out, in_, func, reads, writes, **kw):
        self.S.op("scalar", lambda e: e.activation(out, in_, func, **kw), reads, writes)

    def tt(self, eng, out, in0, in1, op, reads, writes):
        self.S.op(eng, lambda e: e.tensor_tensor(out, in0, in1, op), reads, writes)

    def ts(self, eng, out, in0, s1, s2, op0, op1, reads, writes):
        if op1 is None:
            self.S.op(eng, lambda e: e.tensor_scalar(out, in0, s1, None, op0), reads, writes)
        else:
            self.S.op(eng, lambda e: e.tensor_scalar(out, in0, s1, s2, op0, op1), reads, writes)

    def stt(self, eng, out, in0, scalar, in1, op0, op1, reads, writes):
        self.S.op(eng, lambda e: e.scalar_tensor_tensor(out, in0, scalar, in1, op0, op1), reads, writes)

    def cp(self, eng, out, in_, reads, writes):
        self.S.op(eng, lambda e: e.tensor_copy(out, in_), reads, writes)

    def ms(self, eng, ap, val, reads, writes):
        self.S.op(eng, lambda e: e.memset(ap, val), reads, writes)

    def dma(self, out, in_, reads, writes, q="sync", sk=None, grp=False):
        if sk is None:
            sk = "par" if q == "sync" else "wmisc"
            grp = True
        key = ("dma", q, sk)
        if grp:
            self.S.group_keys.add(key)
        self.S.op(q, lambda e: e.dma_start(out=out, in_=in_), reads, writes, dma=True, semkey=key)

    def build(self):
        nc = self.nc
        L = self.depth
        d = self.d = {}
        d["x"] = self.din("x", [T, D])
        d["c"] = self.din("c", [128, 8])
        d["pos"] = self.din("pos", [1, T], I32)
        d["consts"] = self.din("consts", [128, 1024])
        d["onehot"] = self.din("onehot", [32, T])
        d["w_ada"] = self.din("w_ada", [L, 128, 8, 6 * D])
        d["b_ada"] = self.din("b_ada", [L, 1, 6 * D])
        d["w1"] = self.din("w1", [L, 128, 8, NC1])
        d["b1T"] = self.din("b1T", [L, 128, NCT])
        d["b1row"] = self.din("b1row", [L, 1, 516])
        d["w2"] = self.din("w2", [L, 128, 8, 4096])
        d["b2T"] = self.din("b2T", [L, 128, 32])
        d["wpb"] = self.din("wpb", [L, 128, 2, 128])
        d["pscT"] = self.din("pscT", [L, 128, 2])
        d["convT"] = self.din("convT", [L, 128, 2, 3])
        d["wbr"] = self.din("wbr", [L, 128, 8, D])
        d["wo"] = self.din("wo", [L, 128, 8, D])
        d["lnrow"] = self.din("lnrow", [L, 4, D])
        d["wup"] = self.din("wup", [L, 128, 8, DFF])
        d["bupT"] = self.din("bupT", [L, 128, 32])
        d["wdn"] = self.din("wdn", [L, 128, 32, D])
        d["out"] = nc.dram_tensor("out", [T, D], F32, kind="ExternalOutput").ap()
        d["rotC"] = self.dscr("rotC", [128, T])
        d["rotS"] = self.dscr("rotS", [128, T])
        d["ada_row"] = self.dscr("ada_row", [1, 6 * D])
        d["uT"] = self.dscr("uT", [128, 8, T], BF16)
        d["qk"] = self.dscr("qk", [8, 128, T], BF16)
        d["v"] = self.dscr("v", [T, 512], BF16)
        d["yT"] = self.dscr("yT", [8, 128, T], BF16)
        d["x1"] = self.dscr("x1", [T, D])
        d["x2"] = self.dscr("x2", [T, D])
        if self.debug:
            d["dbg"] = self.dscr("dbg", [128, 1024])

        with contextlib.ExitStack() as top:
            self.S = Sched(nc, top)
            S = self.S

            def sbt(name, shape, dt=F32):
                return top.enter_context(nc.sbuf_tensor(name, list(shape), dt))

            self.ps = [top.enter_context(nc.psum_tensor("ps%d" % i, [128, 512], F32)) for i in range(6)]
            self.psb = [top.enter_context(nc.psum_tensor("psb%d" % i, [128, 1024], BF16)) for i in range(2)]
            self.cf = sbt("cf", [128, 512])
            self.cb = sbt("cbf", [128, 384], BF16)
            self.adaT = sbt("adaT", [128, 48])
            self.lall = sbt("lall", [128, 64, 4])
            self.cl = sbt("cl_hm", [128, 4, 64])
            self.incl = sbt("incl_hm", [128, 4, 64])
            self.cmid = sbt("cmid_hm", [128, 4, 64])
            self.ident_f = self.cf[:, 0:128]
            self.ones_f = self.cf[:, 128:256]
            self.U_f = self.cf[:, 256:384]
            self.ident_b = self.cb[:, 0:128]
            self.tri_b = self.cb[:, 256:384]
            self.epsb = self.cf[:, 420:421]

            self.dma(self.cf[:], d["consts"][:, 0:512], [], ["cf"])
            self.dma(self.cb[:], d["consts"][:, 0:384], [], ["cb"], q="gpsimd")
            self.end_phase()
            if "r" in self.phases or "1" in self.phases:
                self.phase_rot()
            for l in range(L):
                xin = d["x"] if l == 0 else d["x2"]
                xout = d["out"] if l == L - 1 else d["x2"]
                if "0" in self.phases:
                    self.phase0(l)
                if "1" in self.phases:
                    self.phase1(l, xin)
                if "2" in self.phases:
                    self.phase2(l)
                if "3" in self.phases:
                    self.phase3(l, xin)
                if "4" in self.phases:
                    self.phase4(l, xout)
            self.end_phase()

    def end_phase(self):
        self.S.barrier()
        self.S.flush()

    def phase_rot(self):
        nc, d = self.nc, self.d
        TWO_PI = float(2 * np.pi)
        PI = float(np.pi)
        with contextlib.ExitStack() as st:
            def sbt(name, shape, dt=F32):
                return st.enter_context(nc.sbuf_tensor(self.uniq(name), list(shape), dt))
            posi = sbt("r_posi", [128, 512], I32)
            ang = sbt("r_ang", [128, 512])
            kf = sbt("r_kf", [128, 512])
            ki = sbt("r_ki", [128, 512], I32)
            r2 = sbt("r_r2", [128, 512])
            sC = [sbt("r_sC%d" % i, [128, 512]) for i in range(2)]
            sS = [sbt("r_sS%d" % i, [128, 512]) for i in range(2)]
            invf = self.cf[:, 384:385]
            sgn = self.cf[:, 385:386]

            def reduce(buf, tag):
                self.ts("vector", kf[:], buf[:], float(1 / TWO_PI), None, ALU.mult, None, [tag], ["kf"])
                self.cp("vector", ki[:], kf[:], ["kf"], ["ki"])
                self.cp("vector", kf[:], ki[:], ["ki"], ["kf"])
                self.stt("vector", buf[:], kf[:], -TWO_PI, buf[:], ALU.mult, ALU.add, ["kf", tag], [tag])
                self.ts("vector", kf[:], buf[:], PI, -TWO_PI, ALU.is_gt, ALU.mult, [tag], ["kf"])
                self.tt("vector", buf[:], buf[:], kf[:], ALU.add, ["kf", tag], [tag])
                self.ts("vector", kf[:], buf[:], -PI, TWO_PI, ALU.is_lt, ALU.mult, [tag], ["kf"])
                self.tt("vector", buf[:], buf[:], kf[:], ALU.add, ["kf", tag], [tag])

            for g in range(NG):
                sl = slice(g * 512, (g + 1) * 512)
                b = g % 2
                self.dma(posi[:], d["pos"][:, sl].partition_broadcast(128), [], ["posi"], sk="posi")
                self.cp("vector", ang[:], posi[:], ["posi"], ["ang"])
                self.ts("vector", ang[:], ang[:], invf, None, ALU.mult, None, ["ang"], ["ang"])
                reduce(ang, "ang")
                self.act(sS[b][:], ang[:], AF.Sin, ["ang"], [("sS", b)])
                self.ts("vector", r2[:], ang[:], PI / 2, None, ALU.add, None, ["ang"], ["r2"])
                reduce(r2, "r2")
                self.act(sC[b][:], r2[:], AF.Sin, ["r2"], [("sC", b)])
                self.act(sS[b][:], sS[b][:], AF.Identity, [("sS", b)], [("sS", b)], scale=sgn)
                self.dma(d["rotC"][:, sl], sC[b][:], [("sC", b)], [], sk=("sC", b))
                self.dma(d["rotS"][:, sl], sS[b][:], [("sS", b)], [], sk=("sS", b))
            self.end_phase()

    def phase0(self, l):
        nc, d = self.nc, self.d
        with contextlib.ExitStack() as st:
            def sbt(name, shape, dt=F32):
                return st.enter_context(nc.sbuf_tensor(self.uniq(name), list(shape), dt))
            c_sb = sbt("a_c", [128, 8])
            cond_bc = sbt("a_cond", [128, 8, 128], BF16)
            wa = [sbt("a_w%d" % i, [128, 8, 512], BF16) for i in range(2)]
            bada = sbt("a_b", [128, 6 * D])
            ada = sbt("a_ada", [128, 6 * D])
            tmp = sbt("a_tmp", [128, 16, 128])
            self.dma(c_sb[:], d["c"], [], ["c_sb"])
            self.dma(bada[:], d["b_ada"][l].partition_broadcast(128), [], ["bada"])
            self.act(c_sb[:], c_sb[:], AF.Silu, ["c_sb"], ["c_sb"])
            self.cp("vector", cond_bc[:], c_sb[:].unsqueeze(2).to_broadcast([128, 8, 128]), ["c_sb"], ["cond"])
            for blk in range(12):
                b = blk % 2
                cs = slice(blk * 512, (blk + 1) * 512)
                self.dma(wa[b][:], d["w_ada"][l][:, :, cs], [], [("wa", b)], q="gpsimd", sk=("wa", b))
                pb = self.ps[b]
                for k in range(8):
                    self.mm(pb[:], cond_bc[:, k, :], wa[b][:, k, :], k == 0, k == 7, ["cond", ("wa", b)], [("ps", b)])
                self.tt("vector", ada[:, cs], pb[:], bada[:, cs], ALU.add, [("ps", b), "bada"], [("ada", blk)])
            allada = [("ada", blk) for blk in range(12)]
            for p_ in range(3):
                self.tt("vector", tmp[:], ada[:, p_ * 2048:(p_ + 1) * 2048].rearrange("p (a b) -> p a b", b=128),
                        self.ident_f.unsqueeze(1).to_broadcast([128, 16, 128]), ALU.mult, allada, ["atmp"])
                self.S.op("vector", (lambda o, i: (lambda e: e.tensor_reduce(o, i, AX.X, ALU.add)))(
                    self.adaT[:, p_ * 16:(p_ + 1) * 16], tmp[:]), ["atmp"], ["adaT"])
            for c0 in (8, 32):
                self.ts("vector", self.adaT[:, c0:c0 + 8], self.adaT[:, c0:c0 + 8], 1.0, None, ALU.add, None, ["adaT"], ["adaT"])
            self.dma(d["ada_row"], ada[0:1, :], allada, [], sk="adast")
            self.end_phase()

    def ln_stats(self, src, tag, stats, mv, rstd, nb, tg):
        S = self.S
        for hf in range(2):
            S.op("vector", (lambda o, i: (lambda e: e.bn_stats(o, i)))(stats[:, hf, :], src[:, hf * 512:(hf + 1) * 512]),
                 [tag], [(tg, "st", hf)])
        S.op("vector", (lambda o, i: (lambda e: e.bn_aggr(o, i)))(mv[:], stats[:]), [(tg, "st", 0), (tg, "st", 1)], [(tg, "mv")])
        self.act(rstd[:], mv[:, 1:2], AF.Sqrt, [(tg, "mv")], [(tg, "rstd")], bias=self.epsb)
        S.op("vector", (lambda o: (lambda e: e.reciprocal(o, o)))(rstd[:]), [(tg, "rstd")], [(tg, "rstd")])
        self.ts("vector", nb[:], mv[:, 0:1], -1.0, rstd[:], ALU.mult, ALU.mult, [(tg, "mv"), (tg, "rstd")], [(tg, "nb")])

    def phase1(self, l, xin):
        nc, S, d = self.nc, self.S, self.d
        ps, psb = self.ps, self.psb
        with contextlib.ExitStack() as st:
            def sbt(name, shape, dt=F32):
                return st.enter_context(nc.sbuf_tensor(self.uniq(name), list(shape), dt))
            w1 = sbt("p1_w1", [128, 8, NC1], BF16)
            b1T = sbt("p1_b1T", [128, NCT])
            b1Ts = sbt("p1_b1Ts", [128, NCT])
            b1row = sbt("p1_b1row", [128, 516])
            wpb = sbt("p1_wpb", [128, 2, 128], BF16)
            pscT = sbt("p1_pscT", [128, 2])
            convT = sbt("p1_convT", [128, 2, 3])
            x_sb = [sbt("p1_x%d" % i, [128, 1024]) for i in range(2)]
            xn = [sbt("p1_xn%d" % i, [128, 1024], BF16) for i in range(2)]
            stats = [sbt("p1_st%d" % i, [128, 2, 6]) for i in range(2)]
            mv = [sbt("p1_mv%d" % i, [128, 2]) for i in range(2)]
            rstd = [sbt("p1_rs%d" % i, [128, 1]) for i in range(2)]
            nb = [sbt("p1_nb%d" % i, [128, 1]) for i in range(2)]
            uT = [sbt("p1_uT%d" % i, [128, 8, 512], BF16) for i in range(2)]
            stage = [sbt("p1_stg%d" % i, [128, 512], BF16) for i in range(4)]
            pbuf = [sbt("p1_pb%d" % i, [128, 528]) for i in range(2)]
            sA = sbt("p1_sA", [128, 528])
            sB = sbt("p1_sB", [128, 528])
            pooled = sbt("p1_pooled", [128, 512], BF16)
            t16 = sbt("p1_t16", [128, 16])
            cbt = sbt("p1_cb", [128, 512])
            cct = sbt("p1_cc", [128, 512])
            cht = sbt("p1_ch", [128, 512])
            vbuf = [sbt("p1_vb%d" % i, [128, 514]) for i in range(2)]
            zt = sbt("p1_z", [128, 512])
            rc = sbt("p1_rc", [128, 512])
            rs = sbt("p1_rsn", [128, 512])
            a1 = sbt("p1_a1", [128, 512])
            a2 = sbt("p1_a2", [128, 512])
            swp = sbt("p1_swp", [128, 4, 512])
            zt2 = sbt("p1_z2", [128, 512])
            vst = [sbt("p1_vst%d" % i, [128, 512], BF16) for i in range(2)]
            tot = sbt("p1_tot", [128, 4, 64])
            zer = sbt("p1_zer", [128, 64])
            invw = self.cf[:, 386:388]
            icnt = self.cf[:, 388:420]

            for k in range(8):
                self.dma(w1[:, k, :], d["w1"][l][:, k, :], [], [("w1", k)], q="gpsimd", sk=("wk", k))
            self.dma(wpb[:], d["wpb"][l], [], ["wpb"], q="gpsimd")
            self.dma(b1T[:], d["b1T"][l], [], ["b1T"])
            self.dma(b1row[:], d["b1row"][l].partition_broadcast(128), [], ["b1row"])
            self.dma(pscT[:], d["pscT"][l], [], ["pscT"])
            self.dma(convT[:], d["convT"][l], [], ["convT"])
            self.ts("vector", b1Ts[:], b1T[:], 0.125, None, ALU.mult, None, ["b1T"], ["b1Ts"])
            for t in range(2):
                self.ms("gpsimd", pbuf[t][:, 0:16], 0.0, [], [("pbuf", t)])
                self.ms("gpsimd", vbuf[t][:, 0:2], 0.0, [], [("vbuf", t)])
            self.ms("gpsimd", zer[:], 0.0, [], ["zer"])
            allw1 = [("w1", k) for k in range(8)]
            state = {"ps": 0, "stg": 0}

            def next_ps():
                state["ps"] = (state["ps"] + 1) % 4
                return state["ps"]

            def next_stage():
                state["stg"] = (state["stg"] + 1) % 4
                return state["stg"]

            def front(g):
                ub = g % 2
                gs = slice(g * 512, (g + 1) * 512)
                uTr = [("uT", ub, i) for i in range(4)]
                for i in range(4):
                    tb = i % 2
                    r0 = g * 512 + i * 128
                    self.dma(x_sb[tb][:], xin[r0:r0 + 128, :], [], [("x", tb)], sk=("x", tb))
                    self.ln_stats(x_sb[tb], ("x", tb), stats[tb], mv[tb], rstd[tb], nb[tb], ("ln", tb))
                    self.act(xn[tb][:], x_sb[tb][:], AF.Identity, [("x", tb), (("ln", tb), "rstd"), (("ln", tb), "nb")], [("xn", tb)],
                             bias=nb[tb][:], scale=rstd[tb][:])
                    for c in range(8):
                        self.tr(psb[tb][:, c * 128:(c + 1) * 128], xn[tb][:, c * 128:(c + 1) * 128], [("xn", tb), "cb"], [("psb", tb)])
                    for c in range(8):
                        self.act(uT[ub][:, c, i * 128:(i + 1) * 128], psb[tb][:, c * 128:(c + 1) * 128], AF.Identity,
                                 [("psb", tb), "adaT"], [("uT", ub, i)], bias=self.adaT[:, c:c + 1], scale=self.adaT[:, 8 + c:9 + c])
                self.dma(d["uT"][:, :, gs], uT[ub][:], uTr, [], sk=("uTst", ub))

            def body(g, part):
                ub = g % 2
                gs = slice(g * 512, (g + 1) * 512)
                uTr = [("uT", ub, i) for i in range(4)]

                def coltile(j, M=128):
                    b = next_ps()
                    for k in range(8):
                        self.mm(ps[b][0:M, :], w1[:, k, j * 128:j * 128 + M], uT[ub][:, k, :], k == 0, k == 7, uTr + allw1, [("ps", b)])
                    return b

                if part == 1:
                    body_b(g, ub, gs, uTr, coltile)
                    return
                self.dma(rc[:], d["rotC"][:, gs], [], ["rc"], sk="rc")
                self.dma(rs[:], d["rotS"][:, gs], [], ["rs"], sk="rs")
                for w_ in range(4):
                    b = coltile(8 + w_)
                    self.act(swp[:, w_, :], ps[b][:], AF.Identity, [("ps", b), "b1T"], [("swp", w_)], bias=b1T[:, 8 + w_:9 + w_])
                for t in range(2):
                    b = coltile(t)
                    pb_ = pbuf[t]
                    pt = ("pbuf", t)
                    self.act(pb_[:, 16:528], ps[b][:], AF.Identity, [("ps", b), "b1T"], [pt], bias=b1T[:, t:t + 1])
                    self.tt("gpsimd", sA[:, 1:528], pb_[:, 1:528], pb_[:, 0:527], ALU.add, [pt], ["sA"])
                    self.tt("gpsimd", sB[:, 3:528], sA[:, 3:528], sA[:, 1:526], ALU.add, ["sA"], ["sB"])
                    if t == 1:
                        self.tt("gpsimd", sA[:, 7:528], sB[:, 7:528], sB[:, 3:524], ALU.add, ["sB", "sA"], ["sA"])
                        self.tt("gpsimd", sB[:, 15:528], sA[:, 15:528], sA[:, 7:520], ALU.add, ["sA", "sB"], ["sB"])
                    lo, hi = sA, sB
                    if g == 0:
                        for (r0_, src_) in ((0, lo), (64, hi)):
                            self.tt("gpsimd", t16[r0_:r0_ + 64, :], src_[r0_:r0_ + 64, 16:32], icnt[r0_:r0_ + 64, t * 16:(t + 1) * 16], ALU.mult,
                                    ["sA", "sB"], ["t16"])
                    self.act(lo[0:64, 16:528], lo[0:64, 16:528], AF.Identity, ["sA", "sB", "t16"], ["sA"], scale=invw[0:64, t:t + 1])
                    self.act(hi[64:128, 16:528], hi[64:128, 16:528], AF.Identity, ["sA", "sB", "t16"], ["sB"], scale=invw[64:128, t:t + 1])
                    self.tt("gpsimd", pooled[0:64, :], lo[0:64, 16:528], pb_[0:64, 16:528], ALU.subtract, ["sA", pt], ["pooled_lo"])
                    self.tt("gpsimd", pooled[64:128, :], hi[64:128, 16:528], pb_[64:128, 16:528], ALU.subtract, ["sB", pt], ["pooled_hi"])
                    if g == 0:
                        self.tt("gpsimd", pooled[:, 0:16], t16[:, :], pb_[:, 16:32], ALU.subtract,
                                ["t16", pt, "pooled_lo", "pooled_hi"], ["pooled_lo", "pooled_hi"])
                    self.cp("gpsimd", pb_[:, 0:16], pb_[:, 512:528], [pt, "pooled_lo", "pooled_hi", "sA", "sB"], [pt])
                    b2 = next_ps()
                    self.mm(ps[b2][:], wpb[:, t, :], pooled[:], True, True, ["pooled_lo", "pooled_hi", "wpb"], [("ps", b2)])
                    sg = next_stage()
                    self.ts("vector", stage[sg][:], ps[b2][:], pscT[:, t:t + 1], None, ALU.mult, None, [("ps", b2), "pscT"], [("stage", sg)])
                    self.dma(d["yT"][t][:, gs], stage[sg][:], [("stage", sg)], [], sk=("stg", sg))
                for t in range(2):
                    vb_ = vbuf[t]
                    vt = ("vbuf", t)
                    for (j, dst, tg) in ((2 + t, cbt, "cbt"), (4 + t, cct, "cct"), (6 + t, cht, "cht")):
                        b = coltile(j)
                        self.act(dst[:], ps[b][:], AF.Identity, [("ps", b), "b1T"], [tg], bias=b1T[:, j:j + 1])
                    self.tt("gpsimd", vb_[:, 2:514], cct[:], cht[:], ALU.mult, ["cct", "cht"], [vt])
                    self.act(zt[:], vb_[:, 2:514], AF.Identity, [vt, "convT"], ["zt"], scale=convT[:, t, 2:3])
                    self.stt("vector", zt[:], vb_[:, 1:513], convT[:, t, 1:2], zt[:], ALU.mult, ALU.add, [vt, "zt", "convT"], ["zt"])
                    self.stt("vector", zt[:], vb_[:, 0:512], convT[:, t, 0:1], zt[:], ALU.mult, ALU.add, [vt, "zt", "convT"], ["zt"])
                    sg = next_stage()
                    self.tt("gpsimd", stage[sg][:], cbt[:], zt[:], ALU.mult, ["cbt", "zt"], [("stage", sg)])
                    self.cp("gpsimd", vb_[:, 0:2], vb_[:, 512:514], [vt], [vt])
                    self.dma(d["yT"][2 + t][:, gs], stage[sg][:], [("stage", sg)], [], sk=("stg", sg))
            def body_b(g, ub, gs, uTr, coltile):
                for qi in range(8):
                    j = 12 + qi
                    is_q = qi in (0, 1, 4, 5)
                    is_moba = qi < 4
                    sc_ = 0.125 if is_q else 1.0
                    bt = b1Ts if is_q else b1T
                    b = coltile(j)
                    sg = next_stage()
                    self.act(stage[sg][:], ps[b][:], AF.Identity, [("ps", b), "b1T", "b1Ts"], [("stage", sg)], bias=bt[:, j:j + 1], scale=sc_)
                    if is_moba:
                        w_ = qi
                        for hh in range(2):
                            r0_ = hh * 64
                            self.stt("vector", a1[r0_:r0_ + 16, :], ps[b][r0_:r0_ + 16, :], b1T[r0_:r0_ + 16, j:j + 1], rc[r0_:r0_ + 16, :],
                                     ALU.add, ALU.mult, [("ps", b), "b1T", "rc"], ["a1"])
                            self.tt("gpsimd", a2[r0_:r0_ + 16, :], swp[r0_:r0_ + 16, w_, :], rs[r0_:r0_ + 16, :], ALU.mult, [("swp", w_), "rs"], ["a2"])
                            self.tt("gpsimd", a2[r0_:r0_ + 16, :], a2[r0_:r0_ + 16, :], a1[r0_:r0_ + 16, :], ALU.add, ["a1", "a2"], ["a2"])
                            self.act(stage[sg][r0_:r0_ + 16, :], a2[r0_:r0_ + 16, :], AF.Copy, ["a2"], [("stage", sg)], scale=sc_)
                    self.dma(d["qk"][qi][:, gs], stage[sg][:], [("stage", sg)], [], sk=("stg", sg))
                bF = next_ps()
                for i in range(4):
                    b = next_ps()
                    if b == bF:
                        b = next_ps()
                    vs_ = i % 2
                    for k in range(8):
                        self.mm(ps[b][:], uT[ub][:, k, i * 128:(i + 1) * 128], w1[:, k, 2560:3072], k == 0, k == 7, uTr + allw1, [("ps", b)])
                    self.tt("vector", vst[vs_][:], ps[b][:], b1row[:, 0:512], ALU.add, [("ps", b), "b1row"], [("vst", vs_)])
                    r0 = g * 512 + i * 128
                    self.dma(d["v"][r0:r0 + 128, :], vst[vs_][:], [("vst", vs_)], [], sk=("vst", vs_))
                    for k in range(8):
                        self.mm(ps[bF][:, i * 4:(i + 1) * 4], uT[ub][:, k, i * 128:(i + 1) * 128], w1[:, k, 3072:3076], k == 0, k == 7,
                                uTr + allw1, [("ps", bF)])
                self.tt("vector", self.lall[:, g * 4:(g + 1) * 4, :], ps[bF][:, 0:16].rearrange("p (a b) -> p a b", b=4),
                        b1row[:, 512:516].unsqueeze(1).to_broadcast([128, 4, 4]), ALU.add, [("ps", bF), "b1row"], [("lall", g)])

            for g in range(NG):
                front(g)
                body(g, 0)
                body(g, 1)
            lr = [("lall", g) for g in range(NG)]
            lflat = self.lall[:].rearrange("p a b -> p (a b)")
            self.act(lflat, lflat, AF.Exp, lr, ["lall2"], scale=-1.0)
            self.act(lflat, lflat, AF.Ln, ["lall2"], ["lall3"], bias=1.0)
            self.mm(ps[0][:, 0:256], self.U_f, lflat, True, True, ["lall3"], [("ps", 0)])
            self.mm(ps[1][:, 0:256], self.ones_f, lflat, True, True, ["lall3"], [("ps", 1)])
            self.cp("vector", tot[:], ps[1][:, 0:256].rearrange("p (t h) -> p h t", h=4), [("ps", 1)], ["tot"])
            for h in range(4):
                S.op("vector", (lambda o, a, z: (lambda e: e.tensor_tensor_scan(o, a, z, 0.0, ALU.add, ALU.add)))(
                    self.incl[:, h, :], tot[:, h, :], zer[:]), ["tot", "zer"], [("incl", h)])
            inclr = [("incl", h) for h in range(4)]
            self.tt("vector", tot[:], self.incl[:], tot[:], ALU.subtract, inclr + ["tot"], ["tot"])
            self.tt("vector", self.cl[:], ps[0][:, 0:256].rearrange("p (t h) -> p h t", h=4), tot[:], ALU.add, [("ps", 0), "tot"], ["cl"])
            self.tt("vector", self.cmid[:], tot[:], self.incl[:], ALU.add, inclr + ["tot"], ["cmid"])
            self.ts("vector", self.cmid[:], self.cmid[:], 0.5, None, ALU.mult, None, ["cmid"], ["cmid"])
            if self.debug:
                self.dma(d["dbg"][:, 0:256], self.cl[:].rearrange("p a b -> p (a b)"), ["cl"], [], sk="dbg1")
                self.dma(d["dbg"][:, 256:512], self.incl[:].rearrange("p a b -> p (a b)"), inclr, [], sk="dbg2")
            self.end_phase()

    def phase2(self, l):
        nc, S, d = self.nc, self.S, self.d
        ps = self.ps
        with contextlib.ExitStack() as st:
            def sbt(name, shape, dt=F32):
                return st.enter_context(nc.sbuf_tensor(self.uniq(name), list(shape), dt))
            kT = [sbt("p2_kT%d" % i, [128, T], BF16) for i in range(2)]
            V = [sbt("p2_V%d" % i, [128, NTL, 65], BF16) for i in range(2)]
            qT = [sbt("p2_qT%d" % i, [128, 512], BF16) for i in range(2)]
            NPT = 6
            pT = [sbt("p2_pT%d" % i, [128, 512], BF16) for i in range(NPT)]
            o_sb = sbt("p2_o", [64, 512])
            rcr = sbt("p2_rc", [128, 512])
            y_sb = [sbt("p2_y%d" % i, [64, 512], BF16) for i in range(2)]
            sc_sb = [sbt("p2_sc%d" % i, [128, 4, 32]) for i in range(2)]
            m8 = [sbt("p2_m8%d" % i, [128, 4, 8]) for i in range(2)]
            mk = [sbt("p2_mk%d" % i, [128, 4, 96], BF16) for i in range(2)]
            kbf = sbt("p2_kbf", [64, 32])
            kbarT = [sbt("p2_kb%d" % i, [64, 32], BF16) for i in range(2)]
            biasT = [sbt("p2_bias%d" % i, [128, 64]) for i in range(2)]
            psS = [ps[0], ps[1], ps[2], self.psb[1][:].bitcast(F32)]
            NPS = len(psS)
            psO = [ps[3], ps[4]]
            psX = ps[5]
            psM = self.psb[0]

            for i in range(2):
                self.dma(kT[i][64:96, :], d["onehot"], [], [("kToh", i)], q="gpsimd")
            for i in range(2):
                self.ms("gpsimd", V[i][:, :, 64:65], 1.0, [], [("Vone", i)])
                self.ms("gpsimd", mk[i][:], 0.0, [], [("mk", i)])
                self.ms("gpsimd", kT[i][96:97, :], 1.0, [], [("kToh", i)])

            heads = [(ty, h) for ty in range(2) for h in range(4)]
            qcount = [0]

            def load_head(hi):
                ty, h = heads[hi]
                bi = hi % 2
                ktile = (2 if ty == 0 else 6) + h // 2
                r0 = (h % 2) * 64
                self.dma(kT[bi][0:64, :], d["qk"][ktile][r0:r0 + 64, :], [], [("kT", bi)], sk=("kT", bi))
                c0 = ty * 256 + h * 64
                self.dma(V[bi][:, :, 0:64], d["v"][:, c0:c0 + 64].rearrange("(n p) c -> p n c", p=128), [], [("V", bi)], sk=("V", bi))

            def load_q(hi, g):
                ty, h = heads[hi]
                qb = qcount[0] % 2
                qcount[0] += 1
                qtile = (0 if ty == 0 else 4) + h // 2
                r0 = (h % 2) * 64
                self.dma(qT[qb][0:64, :], d["qk"][qtile][r0:r0 + 64, g * 512:(g + 1) * 512], [], [("qT", qb)], sk=("qT", qb))
                return qb

            def prep(hi, g, qb):
                ty, h = heads[hi]
                bi = hi % 2
                if ty == 0:
                    self.ms("gpsimd", sc_sb[qb][:], -1e30, [], [("sc", qb)])
                    for i in range(4):
                        self.mm(psX[:, i * 32:(i + 1) * 32], qT[qb][0:64, i * 128:(i + 1) * 128], kbarT[bi][:, :], True, True,
                                [("qT", qb), ("kbar", bi)], ["psX"])
                    for i in range(4):
                        own = (4 * g + i) // 2
                        if own > 0:
                            self.cp("vector", sc_sb[qb][:, i, 0:own], psX[:, i * 32:i * 32 + own], ["psX", ("sc", qb)], [("sc", qb)])
                    for i in range(4):
                        S.op("vector", (lambda o, a: (lambda e: e.max(o, a)))(m8[qb][:, i, :], sc_sb[qb][:, i, :]), [("sc", qb)], [("m8", qb)])
                    for i in range(4):
                        self.ts("vector", mk[qb][:, i, 64:96], sc_sb[qb][:, i, :], m8[qb][:, i, 2:3], NEG, ALU.is_lt, ALU.mult,
                                [("sc", qb), ("m8", qb)], [("mk", qb)])
                    for i in range(4):
                        own = (4 * g + i) // 2
                        self.ms("gpsimd", mk[qb][:, i, 64 + own:65 + own], 0.0, [("mk", qb)], [("mk", qb)])
                    for i in range(4):
                        self.tr(psM[0:96, i * 128:(i + 1) * 128], mk[qb][:, i, 0:96], [("mk", qb), "cb"], ["psM"])
                    self.act(qT[qb][64:96, :], psM[64:96, 0:512], AF.Copy, ["psM", ("qT", qb)], [("qT", qb)])
                else:
                    n = 4 * g + 4
                    self.ts("vector", biasT[qb][:, 0:n], self.cl[:, h, 0:n], self.incl[:, h, 4 * g + 1:4 * g + 2], None, ALU.subtract, None,
                            [], [("biasT", qb)])
                    if h == 0 and g < 2:
                        self.ms("gpsimd", qT[qb][64:96, :], 0.0, [("qT", qb)], [("qT", qb)])
                    self.ts("vector", qT[qb][96:97, :].rearrange("p (a b) -> p a b", b=128),
                            self.cmid[96:97, h, 4 * g:4 * g + 4].unsqueeze(2).to_broadcast([1, 4, 128]),
                            -1.0, self.incl[96:97, h, 4 * g + 1:4 * g + 2], ALU.mult, ALU.add, [("qT", qb)], [("qT", qb)])

            def main(hi, g, qb):
                ty, h = heads[hi]
                bi = hi % 2
                K = 96 if ty == 0 else 97
                nkt = 4 * g + 4
                ob = g % 2

                def emit_S(kt):
                    nq0 = max(0, kt - 4 * g) * 128
                    sb_ = kt % NPS
                    pb_ = kt % NPT
                    self.mm(psS[sb_][:, nq0:512], kT[bi][0:K, kt * 128:(kt + 1) * 128], qT[qb][0:K, nq0:512], True, True,
                            [("kT", bi), ("kToh", bi), ("qT", qb)], [("psS", sb_)])
                    if ty == 0:
                        self.act(pT[pb_][:, nq0:512], psS[sb_][:, nq0:512], AF.Exp, [("psS", sb_)], [("pT", pb_)])
                    else:
                        self.act(pT[pb_][:, nq0:512], psS[sb_][:, nq0:512], AF.Exp, [("psS", sb_), ("biasT", qb)], [("pT", pb_)],
                                 bias=biasT[qb][:, kt:kt + 1])
                    if kt >= 4 * g:
                        self.tt("gpsimd", pT[pb_][:, nq0:nq0 + 128], pT[pb_][:, nq0:nq0 + 128], self.tri_b, ALU.mult, [("pT", pb_)], [("pT", pb_)])

                def emit_PV(kt):
                    nq0 = max(0, kt - 4 * g) * 128
                    pb_ = kt % NPT
                    self.mm(psO[ob][0:65, nq0:512], V[bi][:, kt, 0:65], pT[pb_][:, nq0:512], kt == 0, kt == nkt - 1,
                            [("pT", pb_), ("V", bi), ("Vone", bi)], [("psO", ob)])

                SK = 3
                for kt in range(nkt + SK):
                    if kt < nkt:
                        emit_S(kt)
                    if kt >= SK:
                        emit_PV(kt - SK)
                self.act(o_sb[:], psO[ob][0:64, :], AF.Copy, [("psO", ob)], ["o_sb"])
                S.op("vector", (lambda o, a: (lambda e: e.reciprocal(o, a)))(rcr[64:65, :], psO[ob][64:65, :]), [("psO", ob)], ["rcr"])
                self.mm(psX[0:64, :], self.ones_f[64:65, 0:64], rcr[64:65, :], True, True, ["rcr"], ["psX"])
                yb = g % 2
                self.tt("vector", y_sb[yb][:], o_sb[:], psX[0:64, :], ALU.mult, ["o_sb", "psX"], [("y", yb)])
                ytile = 4 + ty * 2 + h // 2
                r0 = (h % 2) * 64
                self.dma(d["yT"][ytile][r0:r0 + 64, g * 512:(g + 1) * 512], y_sb[yb][:], [("y", yb)], [], sk=("y", yb))

            load_head(0)
            for hi in range(8):
                ty, h = heads[hi]
                bi = hi % 2
                if hi + 1 < 8:
                    load_head(hi + 1)
                if ty == 0:
                    S.op("vector", (lambda o, a: (lambda e: e.tensor_reduce(o, a, AX.X, ALU.add)))(
                        kbf[:], kT[bi][0:64, :].rearrange("p (n c) -> p n c", c=256)), [("kT", bi)], ["kbf"])
                    self.ts("vector", kbarT[bi][:], kbf[:], 1.0 / 256.0, None, ALU.mult, None, ["kbf"], [("kbar", bi)])
                qb = load_q(hi, 0)
                prep(hi, 0, qb)
                for g in range(NG):
                    qb_n = None
                    if g + 1 < NG:
                        qb_n = load_q(hi, g + 1)
                        prep(hi, g + 1, qb_n)
                    main(hi, g, qb)
                    qb = qb_n
            self.end_phase()

    def residual_ln(self, pfx, pA, pB, tagA, tagB, xt, tagx, gbc, lng, lnb, tmp, stats, mv, rstd, nb, dst_ap):
        t = (pfx, "tmp")
        self.tt("vector", tmp[:, 0:512], pA[:], gbc[:, 0:512], ALU.mult, [tagA, (pfx, "gbc")], [(pfx, "tmpa"), t])
        self.tt("vector", tmp[:, 512:1024], pB[:], gbc[:, 512:1024], ALU.mult, [tagB, (pfx, "gbc")], [(pfx, "tmpb"), t])
        self.act(xt[:], xt[:], AF.Copy, [tagx], [tagx], scale=float(ALPHA))
        self.tt("gpsimd", tmp[:], tmp[:], xt[:], ALU.add, [(pfx, "tmpa"), (pfx, "tmpb"), tagx], [t])
        self.ln_stats(tmp, t, stats, mv, rstd, nb, (pfx, "ln"))
        self.act(tmp[:], tmp[:], AF.Identity, [t, ((pfx, "ln"), "rstd"), ((pfx, "ln"), "nb")], [t], bias=nb[:], scale=rstd[:])
        self.tt("gpsimd", tmp[:], tmp[:], lng[:], ALU.mult, [t, (pfx, "lng")], [t])
        self.tt("gpsimd", tmp[:], tmp[:], lnb[:], ALU.add, [t, (pfx, "lnb")], [t])
        self.dma(dst_ap, tmp[:], [t], [], sk=("xst", pfx))

    def phase3(self, l, xin):
        nc, S, d = self.nc, self.S, self.d
        ps = self.ps
        with contextlib.ExitStack() as st:
            def sbt(name, shape, dt=F32):
                return st.enter_context(nc.sbuf_tensor(self.uniq(name), list(shape), dt))
            w2 = sbt("p3_w2", [128, 8, 4096], BF16)
            wbr = sbt("p3_wbr", [128, 8, D], BF16)
            wo = sbt("p3_wo", [128, 8, D], BF16)
            b2T = sbt("p3_b2T", [128, 32])
            g1 = sbt("p3_g1", [128, D])
            lng = sbt("p3_lng", [128, D])
            lnb = sbt("p3_lnb", [128, D])
            uTs = [sbt("p3_uT%d" % i, [128, 8, 512], BF16) for i in range(2)]
            yTs = [sbt("p3_yT%d" % i, [128, 8, 512], BF16) for i in range(2)]
            xt = [sbt("p3_xt%d" % i, [128, D]) for i in range(2)]
            mT = sbt("p3_mT", [128, 8, 512], BF16)
            gate = [sbt("p3_gate%d" % i, [128, 512]) for i in range(2)]
            term = [sbt("p3_term%d" % i, [128, 512]) for i in range(2)]
            acc = sbt("p3_acc", [128, 512])
            tmp = [sbt("p3_tmp%d" % i, [128, D]) for i in range(2)]
            stats = [sbt("p3_st%d" % i, [128, 2, 6]) for i in range(2)]
            mv = [sbt("p3_mv%d" % i, [128, 2]) for i in range(2)]
            rstd = [sbt("p3_rs%d" % i, [128, 1]) for i in range(2)]
            nb = [sbt("p3_nb%d" % i, [128, 1]) for i in range(2)]
            for k in range(8):
                self.dma(w2[:, k, :], d["w2"][l][:, k, :], [], [("w2", k)], q="gpsimd", sk=("wk", k))
            self.dma(wbr[:], d["wbr"][l], [], ["wbr"], q="gpsimd")
            self.dma(wo[:], d["wo"][l], [], ["wo"], q="gpsimd")
            self.dma(b2T[:], d["b2T"][l], [], ["b2T"])
            self.dma(g1[:], d["ada_row"][:, 2048:3072].partition_broadcast(128), [], [("p3a", "gbc"), ("p3b", "gbc")])
            self.dma(lng[:], d["lnrow"][l][0:1, :].partition_broadcast(128), [], [("p3a", "lng"), ("p3b", "lng")])
            self.dma(lnb[:], d["lnrow"][l][1:2, :].partition_broadcast(128), [], [("p3a", "lnb"), ("p3b", "lnb")])
            allw2 = [("w2", k) for k in range(8)]
            cnt = [0]

            def load(g):
                gs = slice(g * 512, (g + 1) * 512)
                sl_ = g % 2
                self.dma(uTs[sl_][:], d["uT"][:, :, gs], [], [("uT", sl_)], sk=("p3uT", sl_))
                self.dma(yTs[sl_][:], d["yT"][:, :, gs].rearrange("n p t -> p n t"), [], [("yT", sl_)], sk=("p3yT", sl_))

            def group(g):
                sl_ = g % 2
                uT, yT = uTs[sl_], yTs[sl_]
                if g + 1 < NG:
                    load(g + 1)
                for dc in range(8):
                    for n in range(4):
                        c_ = cnt[0] % 2
                        cnt[0] += 1
                        pg = ps[c_]
                        pbr = ps[2 + c_]
                        for k in range(8):
                            self.mm(pg[:], w2[:, k, n * 1024 + dc * 128:n * 1024 + (dc + 1) * 128], uT[:, k, :], k == 0, k == 7,
                                    [("uT", sl_)] + allw2, [("psg", c_)])
                        for c in range(2):
                            self.mm(pbr[:], wbr[:, n * 2 + c, dc * 128:(dc + 1) * 128], yT[:, n * 2 + c, :], c == 0, c == 1,
                                    [("yT", sl_), "wbr"], [("psbr", c_)])
                        self.act(gate[c_][:], pg[:], AF.Sigmoid, [("psg", c_), "b2T"], [("gate", c_)], bias=b2T[:, n * 8 + dc:n * 8 + dc + 1])
                        if n == 0:
                            self.tt("vector", acc[:], gate[c_][:], pbr[:], ALU.mult, [("gate", c_), ("psbr", c_)], ["acc"])
                        else:
                            self.tt("vector", term[c_][:], gate[c_][:], pbr[:], ALU.mult, [("gate", c_), ("psbr", c_)], [("term", c_)])
                            if n < 3:
                                self.tt("gpsimd", acc[:], acc[:], term[c_][:], ALU.add, ["acc", ("term", c_)], ["acc"])
                            else:
                                self.tt("gpsimd", mT[:, dc, :], acc[:], term[c_][:], ALU.add, ["acc", ("term", c_)], [("mT", dc)])
                mTr = [("mT", dc) for dc in range(8)]
                for i in range(4):
                    tb = i % 2
                    pfx = "p3a" if tb == 0 else "p3b"
                    r0 = g * 512 + i * 128
                    self.dma(xt[tb][:], xin[r0:r0 + 128, :], [], [("p3x", tb)], sk=("p3x", tb))
                    pA, pB = ps[4], ps[5]
                    for hf, pp, tg in ((0, pA, "p3psA"), (1, pB, "p3psB")):
                        for k in range(8):
                            self.mm(pp[:], mT[:, k, i * 128:(i + 1) * 128], wo[:, k, hf * 512:(hf + 1) * 512], k == 0, k == 7,
                                    mTr + ["wo"], [tg])
                    self.residual_ln(pfx, pA, pB, "p3psA", "p3psB", xt[tb], ("p3x", tb), g1, lng, lnb, tmp[tb], stats[tb], mv[tb],
                                     rstd[tb], nb[tb], d["x1"][r0:r0 + 128, :])

            load(0)
            for g in range(NG):
                group(g)
            self.end_phase()

    def phase4(self, l, xout):
        nc, S, d = self.nc, self.S, self.d
        ps, psb = self.ps, self.psb
        with contextlib.ExitStack() as st:
            def sbt(name, shape, dt=F32):
                return st.enter_context(nc.sbuf_tensor(self.uniq(name), list(shape), dt))
            wup = sbt("p4_wup", [128, 8, DFF], BF16)
            wdn = sbt("p4_wdn", [128, 32, D], BF16)
            bupT = sbt("p4_bupT", [128, 32])
            g2 = sbt("p4_g2", [128, D])
            lng = sbt("p4_lng", [128, D])
            lnb = sbt("p4_lnb", [128, D])
            u2T = sbt("p4_u2T", [128, 8, 512], BF16)
            hT = sbt("p4_hT", [128, 32, 512], BF16)
            xt = sbt("p4_xt", [128, D])
            xn = sbt("p4_xn", [128, D], BF16)
            rt = sbt("p4_rt", [128, 512])
            tmp = sbt("p4_tmp", [128, D])
            stats = sbt("p4_st", [128, 2, 6])
            mv = sbt("p4_mv", [128, 2])
            rstd = sbt("p4_rs", [128, 1])
            nb = sbt("p4_nb", [128, 1])
            for k in range(8):
                self.dma(wup[:, k, :], d["wup"][l][:, k, :], [], [("wup", k)], q="gpsimd", sk=("wk", k))
            for k in range(0, 32, 4):
                self.dma(wdn[:, k:k + 4, :], d["wdn"][l][:, k:k + 4, :], [], [("wdn", k)], q="gpsimd", sk=("wk2", k))
            self.dma(bupT[:], d["bupT"][l], [], ["bupT"])
            self.dma(g2[:], d["ada_row"][:, 5120:6144].partition_broadcast(128), [], [("p4", "gbc")])
            self.dma(lng[:], d["lnrow"][l][2:3, :].partition_broadcast(128), [], [("p4", "lng")])
            self.dma(lnb[:], d["lnrow"][l][3:4, :].partition_broadcast(128), [], [("p4", "lnb")])
            allwup = [("wup", k) for k in range(8)]
            allwdn = [("wdn", k) for k in range(0, 32, 4)]

            def group(g):
                for i in range(4):
                    tb = i % 2
                    r0 = g * 512 + i * 128
                    self.dma(xt[:], d["x1"][r0:r0 + 128, :], [], ["p4x"], sk="p4x")
                    self.ln_stats(xt, "p4x", stats, mv, rstd, nb, ("p4", "ln"))
                    self.act(xn[:], xt[:], AF.Identity, ["p4x", (("p4", "ln"), "rstd"), (("p4", "ln"), "nb")], ["p4xn"],
                             bias=nb[:], scale=rstd[:])
                    for c in range(8):
                        self.tr(psb[tb][:, c * 128:(c + 1) * 128], xn[:, c * 128:(c + 1) * 128], ["p4xn", "cb"], [("psb", tb)])
                    for c in range(8):
                        self.act(u2T[:, c, i * 128:(i + 1) * 128], psb[tb][:, c * 128:(c + 1) * 128], AF.Identity,
                                 [("psb", tb), "adaT"], [("u2T", i)], bias=self.adaT[:, 24 + c:25 + c], scale=self.adaT[:, 32 + c:33 + c])
                u2r = [("u2T", i) for i in range(4)]
                for fc in range(32):
                    c_ = fc % 2
                    pu = ps[c_]
                    for k in range(8):
                        self.mm(pu[:], wup[:, k, fc * 128:(fc + 1) * 128], u2T[:, k, :], k == 0, k == 7, u2r + allwup, [("psu", c_)])
                    self.ts("vector", rt[:], pu[:], bupT[:, fc:fc + 1], 0.0, ALU.add, ALU.max, [("psu", c_), "bupT"], ["rt"])
                    self.act(hT[:, fc, :], rt[:], AF.Square, ["rt"], [("hT", fc)])
                hr = [("hT", fc) for fc in range(32)]
                for i in range(4):
                    r0 = g * 512 + i * 128
                    par = i % 2
                    pA, pB = ps[2 + par * 2], ps[3 + par * 2]
                    tgA, tgB = ("p4psA", par), ("p4psB", par)
                    for hf, pp, tg in ((0, pA, tgA), (1, pB, tgB)):
                        for fc in range(32):
                            self.mm(pp[:], hT[:, fc, i * 128:(i + 1) * 128], wdn[:, fc, hf * 512:(hf + 1) * 512], fc == 0, fc == 31,
                                    hr + allwdn, [tg])
                    self.dma(xt[:], d["x1"][r0:r0 + 128, :], [], ["p4x"], sk="p4x")
                    self.residual_ln("p4", pA, pB, tgA, tgB, xt, "p4x", g2, lng, lnb, tmp, stats, mv, rstd, nb, xout[r0:r0 + 128, :])

            for g in range(NG):
                group(g)
            self.end_phase()


def prep_weights(w_ada, b_ada, w_in, b_in, w_pool, pool_scale, conv_w, w_branch, w_o, ln1_g, ln1_b,
                 w_up, b_up, w_down, ln2_g, ln2_b, depth=DEPTH):
    L = depth
    f = lambda a: np.ascontiguousarray(np.asarray(a, dtype=np.float32))
    w_in = f(w_in)
    b_in = f(b_in)
    cols = _cols1()
    m = {}
    m["w_ada"] = f(f(w_ada)[:L].reshape(L, 8, 128, 6 * D).transpose(0, 2, 1, 3))
    m["b_ada"] = f(f(b_ada)[:L].reshape(L, 1, 6 * D))
    m["w1"] = f(w_in[:L][:, :, cols].reshape(L, 8, 128, NC1).transpose(0, 2, 1, 3))
    m["b1T"] = f(b_in[:L][:, cols[:NCT * 128]].reshape(L, NCT, 128).transpose(0, 2, 1))
    m["b1row"] = f(b_in[:L][:, cols[NCT * 128:]].reshape(L, 1, 516))
    m["w2"] = f(w_in[:L][:, :, 2564:].reshape(L, 8, 128, 4096).transpose(0, 2, 1, 3))
    m["b2T"] = f(b_in[:L][:, 2564:].reshape(L, 32, 128).transpose(0, 2, 1))
    wp = f(w_pool)[:L]
    wpb = np.zeros((L, 128, 2, 128), np.float32)
    for t in range(2):
        wpb[:, 0:64, t, 0:64] = wp[:, 2 * t]
        wpb[:, 64:128, t, 64:128] = wp[:, 2 * t + 1]
    m["wpb"] = wpb
    m["pscT"] = f(f(pool_scale)[:L].reshape(L, 2, 128).transpose(0, 2, 1))
    m["convT"] = f(f(conv_w)[:L].reshape(L, 3, 2, 128).transpose(0, 3, 2, 1))
    m["wbr"] = f(f(w_branch)[:L].reshape(L, 4, 2, 128, D).transpose(0, 3, 1, 2, 4).reshape(L, 128, 8, D))
    m["wo"] = f(f(w_o)[:L].reshape(L, 8, 128, D).transpose(0, 2, 1, 3))
    m["lnrow"] = f(np.stack([f(ln1_g)[:L], f(ln1_b)[:L], f(ln2_g)[:L], f(ln2_b)[:L]], axis=1))
    m["wup"] = f(f(w_up)[:L].reshape(L, 8, 128, DFF).transpose(0, 2, 1, 3))
    m["bupT"] = f(f(b_up)[:L].reshape(L, 32, 128).transpose(0, 2, 1))
    m["wdn"] = f(f(w_down)[:L].reshape(L, 32, 128, D).transpose(0, 2, 1, 3))
    m["consts"] = _consts()
    m["onehot"] = _onehot()
    return m


_PROG_CACHE = {}


def get_prog(depth=DEPTH, debug=False, phases="01234"):
    key = (depth, debug, phases)
    if key not in _PROG_CACHE:
        _PROG_CACHE[key] = Prog(depth, debug, phases)
    return _PROG_CACHE[key]


def kernel(x, c, positions, w_ada, b_ada, w_in, b_in, w_pool, pool_scale, conv_w, w_branch, w_o,
           ln1_g, ln1_b, w_up, b_up, w_down, ln2_g, ln2_b):
    prog = get_prog()
    wm = prep_weights(w_ada, b_ada, w_in, b_in, w_pool, pool_scale, conv_w, w_branch, w_o, ln1_g, ln1_b,
                      w_up, b_up, w_down, ln2_g, ln2_b)
    x = np.asarray(x, dtype=np.float32)
    c = np.asarray(c, dtype=np.float32)
    positions = np.asarray(positions, dtype=np.int32)
    in_maps = []
    for b in range(BATCH):
        m = dict(wm)
        m["x"] = np.ascontiguousarray(x[b])
        m["c"] = np.ascontiguousarray(c[b].reshape(8, 128).T)
        m["pos"] = np.ascontiguousarray(positions[b].reshape(1, T))
        in_maps.append(m)
    res = run_bass_kernel_spmd(prog.nc, in_maps, core_ids=list(range(BATCH)))
    out = np.stack([np.asarray(res.results[b]["out"], dtype=np.float32) for b in range(BATCH)], axis=0)
    return out
```

```python
import contextlib
import math
import numpy as np
import concourse.bass as bass
import concourse.mybir as mybir
from concourse.bass_utils import run_bass_kernel_spmd

F32 = mybir.dt.float32
BF16 = mybir.dt.bfloat16
I32 = mybir.dt.int32
AF = mybir.ActivationFunctionType
ALU = mybir.AluOpType
AX = mybir.AxisListType

D = 1024
SEQ = 8192
BATCH = 4
DEPTH = 4
DFF = 4096
HD = 64
IN_WIDTH = 6660
ALPHA = (2 * DEPTH) ** 0.25
LN_EPS = 1e-5
ROPE_THETA = 500000.0
T = SEQ
G = 512
NG = T // G
NTL = T // 128
NC1 = 20 * 128 + 512 + 4
NCT = 20
NEG = -30000.0
ENGINES = ("tensor", "vector", "scalar", "gpsimd", "sync")


class Op:
    __slots__ = ("eng", "fn", "deps", "is_dma", "semkey", "ms", "needed")

    def __init__(self, eng, fn, is_dma, semkey):
        self.eng = eng
        self.fn = fn
        self.deps = []
        self.is_dma = is_dma
        self.semkey = semkey
        self.ms = None
        self.needed = False


class Sched:
    def __init__(self, nc, stack):
        self.nc = nc
        self.stack = stack
        self.pending = {e: [] for e in ENGINES}
        self.last_w = {}
        self.readers = {}
        self.sem_count = {}
        self.sems = {}
        self.last_by_key = {}
        self.waited = {e: {} for e in ENGINES}
        self.n_inst = 0
        self.group_keys = set()
        self.trace = {}
        import os
        for i in range(int(os.environ.get("SEM_SKIP", "0"))):
            self.stack.enter_context(nc.semaphore("dummy%d" % i))

    def op(self, eng, fn, reads=(), writes=(), dma=False, semkey=None):
        semkey = (semkey or ("dma", eng)) if dma else ("eng", eng)
        o = Op(eng, fn, dma, semkey)
        deps = set()

        def same_stream(w):
            return (not dma) and (not w.is_dma) and w.eng == eng

        for r in reads:
            w = self.last_w.get(r)
            if w is not None:
                if same_stream(w) and eng == "tensor":
                    continue
                deps.add(w)
        for r in writes:
            w = self.last_w.get(r)
            if w is not None and not same_stream(w):
                deps.add(w)
            for rd in self.readers.get(r, ()):
                if not same_stream(rd):
                    deps.add(rd)
        for r in reads:
            self.readers.setdefault(r, []).append(o)
        for r in writes:
            self.last_w[r] = o
            self.readers[r] = []
        deps.discard(o)
        o.deps = list(deps)
        for d in o.deps:
            d.needed = True
        self.pending[eng].append(o)
        self.last_by_key[semkey] = o
        return o

    def barrier(self):
        targets = list(self.last_by_key.values())
        for t in targets:
            t.needed = True
        for e in ENGINES:
            b = Op(e, None, False, ("eng", e))
            b.deps = targets
            self.pending[e].append(b)
        self.last_w = {}
        self.readers = {}

    def flush(self):
        nc = self.nc
        for e in ENGINES:
            for o in self.pending[e]:
                if o.fn is None:
                    continue
                if o.is_dma:
                    self.sem_count[o.semkey] = self.sem_count.get(o.semkey, 0) + 16
                    o.ms = self.sem_count[o.semkey]
                elif o.needed:
                    self.sem_count[o.semkey] = self.sem_count.get(o.semkey, 0) + 1
                    o.ms = self.sem_count[o.semkey]
        for gk in self.group_keys:
            grp = [o for e in ENGINES for o in self.pending[e] if o.fn is not None and o.semkey == gk]
            if grp:
                mx = max(o.ms for o in grp)
                for o in grp:
                    o.ms = mx
        for k in self.sem_count:
            if k not in self.sems:
                name = "s%d_" % len(self.sems) + "".join(ch if ch.isalnum() else "_" for ch in str(k))[:40]
                self.sems[k] = self.stack.enter_context(nc.semaphore(name))
        with nc.Block() as block:
            for e in ENGINES:
                if self.pending[e]:
                    getattr(block, e)(self._mk_body(e, self.pending[e]))
        self.pending = {e: [] for e in ENGINES}

    def _mk_body(self, e, ops):
        sems = self.sems
        waited = self.waited[e]

        tr = self.trace.setdefault(e, [])

        def body(eng):
            for o in ops:
                tr.append(o)
                need = {}
                for d in o.deps:
                    if d.ms is None:
                        continue
                    if need.get(d.semkey, 0) < d.ms:
                        need[d.semkey] = d.ms
                for k, v in need.items():
                    if waited.get(k, 0) >= v:
                        continue
                    eng.wait_ge(sems[k], v)
                    waited[k] = v
                    self.n_inst += 1
                if o.fn is None:
                    continue
                ins = o.fn(eng)
                self.n_inst += 1
                if o.is_dma:
                    ins.then_inc(sems[o.semkey], 16)
                elif o.ms is not None:
                    ins.then_inc(sems[o.semkey], 1)
        return body


def _cols1():
    cols = []
    cols += list(range(0, 256))
    cols += list(range(256, 1024))
    for base in (1024, 1280):
        for pair in range(2):
            for r in range(128):
                hh, dd = r // 64, r % 64
                src = ((dd + 8) % 16) if dd < 16 else dd
                cols.append(base + (pair * 2 + hh) * 64 + src)
    cols += list(range(1024, 1280))
    cols += list(range(1280, 1536))
    cols += list(range(1792, 2048))
    cols += list(range(2048, 2304))
    cols += list(range(1536, 1792))
    cols += list(range(2304, 2560))
    cols += list(range(2560, 2564))
    assert len(cols) == NC1
    return np.asarray(cols, dtype=np.int64)


def _consts():
    c = np.zeros((128, 1024), np.float32)
    p = np.arange(128)
    c[:, 0:128] = np.eye(128, dtype=np.float32)
    c[:, 128:256] = 1.0
    c[:, 256:384] = (p[:, None] <= p[None, :]).astype(np.float32)
    d = p % 64
    inv = ROPE_THETA ** (-(2.0 * (d % 8)) / 16.0)
    c[:, 384] = np.where(d < 16, inv, 0.0)
    c[:, 385] = np.where(d < 8, -1.0, np.where(d < 16, 1.0, 0.0))
    w_lo = np.array([2.0, 8.0])
    w_hi = np.array([4.0, 16.0])
    for t in range(2):
        w = np.where(p < 64, w_lo[t], w_hi[t])
        c[:, 386 + t] = 1.0 / w
        for j in range(16):
            c[:, 388 + t * 16 + j] = 1.0 / np.minimum(j + 1.0, w)
    c[:, 420] = LN_EPS
    return c


def _onehot():
    oh = np.zeros((32, T), np.float32)
    for n in range(32):
        oh[n, n * 256:(n + 1) * 256] = 1.0
    return oh


class Prog:
    def __init__(self, depth=DEPTH, debug=False, phases="01234"):
        self.depth = depth
        self.debug = debug
        self.phases = phases
        self.nc = bass.Bass("TRN2", target_bir_lowering=False)
        self.build()

    def din(self, name, shape, dt=F32):
        return self.nc.dram_tensor(name, list(shape), dt, kind="ExternalInput").ap()

    def dscr(self, name, shape, dt=F32):
        if self.debug:
            return self.nc.dram_tensor(name, list(shape), dt, kind="ExternalOutput").ap()
        return self.nc.dram_tensor(name, list(shape), dt).ap()

    def uniq(self, name):
        self._uid = getattr(self, "_uid", 0) + 1
        return "%s_u%d" % (name, self._uid)

    def mm(self, out, lhsT, rhs, start, stop, reads, writes):
        self.S.op("tensor", lambda e: e.matmul(out, lhsT, rhs, start=start, stop=stop), reads, writes)

    def tr(self, out, in_, reads, writes):
        ident = self.ident_b
        self.S.op("tensor", lambda e: e.transpose(out, in_, ident), reads, writes)

    def act(self, out, in_, func, reads, writes, **kw):
        self.S.op("scalar", lambda e: e.activation(out, in_, func, **kw), reads, writes)

    def tt(self, eng, out, in0, in1, op, reads, writes):
        self.S.op(eng, lambda e: e.tensor_tensor(out, in0, in1, op), reads, writes)

    def ts(self, eng, out, in0, s1, s2, op0, op1, reads, writes):
        if op1 is None:
            self.S.op(eng, lambda e: e.tensor_scalar(out, in0, s1, None, op0), reads, writes)
        else:
            self.S.op(eng, lambda e: e.tensor_scalar(out, in0, s1, s2, op0, op1), reads, writes)

    def stt(self, eng, out, in0, scalar, in1, op0, op1, reads, writes):
        self.S.op(eng, lambda e: e.scalar_tensor_tensor(out, in0, scalar, in1, op0, op1), reads, writes)

    def cp(self, eng, out, in_, reads, writes):
        self.S.op(eng, lambda e: e.tensor_copy(out, in_), reads, writes)

    def ms(self, eng, ap, val, reads, writes):
        self.S.op(eng, lambda e: e.memset(ap, val), reads, writes)

    def dma(self, out, in_, reads, writes, q="sync", sk=None, grp=False):
        if sk is None:
            sk = "par" if q == "sync" else "wmisc"
            grp = True
        key = ("dma", q, sk)
        if grp:
            self.S.group_keys.add(key)
        self.S.op(q, lambda e: e.dma_start(out=out, in_=in_), reads, writes, dma=True, semkey=key)

    def build(self):
        nc = self.nc
        L = self.depth
        d = self.d = {}
        d["x"] = self.din("x", [T, D])
        d["c"] = self.din("c", [128, 8])
        d["pos"] = self.din("pos", [1, T], I32)
        d["consts"] = self.din("consts", [128, 1024])
        d["onehot"] = self.din("onehot", [32, T])
        d["w_ada"] = self.din("w_ada", [L, 128, 8, 6 * D])
        d["b_ada"] = self.din("b_ada", [L, 1, 6 * D])
        d["w1"] = self.din("w1", [L, 128, 8, NC1])
        d["b1T"] = self.din("b1T", [L, 128, NCT])
        d["b1row"] = self.din("b1row", [L, 1, 516])
        d["w2"] = self.din("w2", [L, 128, 8, 4096])
        d["b2T"] = self.din("b2T", [L, 128, 32])
        d["wpb"] = self.din("wpb", [L, 128, 2, 128])
        d["pscT"] = self.din("pscT", [L, 128, 2])
        d["convT"] = self.din("convT", [L, 128, 2, 3])
        d["wbr"] = self.din("wbr", [L, 128, 8, D])
        d["wo"] = self.din("wo", [L, 128, 8, D])
        d["lnrow"] = self.din("lnrow", [L, 4, D])
        d["wup"] = self.din("wup", [L, 128, 8, DFF])
        d["bupT"] = self.din("bupT", [L, 128, 32])
        d["wdn"] = self.din("wdn", [L, 128, 32, D])
        d["out"] = nc.dram_tensor("out", [T, D], F32, kind="ExternalOutput").ap()
        d["rotC"] = self.dscr("rotC", [128, T])
        d["rotS"] = self.dscr("rotS", [128, T])
        d["ada_row"] = self.dscr("ada_row", [1, 6 * D])
        d["uT"] = self.dscr("uT", [128, 8, T], BF16)
        d["qk"] = self.dscr("qk", [8, 128, T], BF16)
        d["v"] = self.dscr("v", [T, 512], BF16)
        d["yT"] = self.dscr("yT", [8, 128, T], BF16)
        d["x1"] = self.dscr("x1", [T, D])
        d["x2"] = self.dscr("x2", [T, D])
        if self.debug:
            d["dbg"] = self.dscr("dbg", [128, 1024])

        with contextlib.ExitStack() as top:
            self.S = Sched(nc, top)
            S = self.S

            def sbt(name, shape, dt=F32):
                return top.enter_context(nc.sbuf_tensor(name, list(shape), dt))

            self.ps = [top.enter_context(nc.psum_tensor("ps%d" % i, [128, 512], F32)) for i in range(6)]
            self.psb = [top.enter_context(nc.psum_tensor("psb%d" % i, [128, 1024], BF16)) for i in range(2)]
            self.cf = sbt("cf", [128, 512])
            self.cb = sbt("cbf", [128, 384], BF16)
            self.adaT = sbt("adaT", [128, 48])
            self.lall = sbt("lall", [128, 64, 4])
            self.cl = sbt("cl_hm", [128, 4, 64])
            self.incl = sbt("incl_hm", [128, 4, 64])
            self.cmid = sbt("cmid_hm", [128, 4, 64])
            self.ident_f = self.cf[:, 0:128]
            self.ones_f = self.cf[:, 128:256]
            self.U_f = self.cf[:, 256:384]
            self.ident_b = self.cb[:, 0:128]
            self.tri_b = self.cb[:, 256:384]
            self.epsb = self.cf[:, 420:421]

            self.dma(self.cf[:], d["consts"][:, 0:512], [], ["cf"])
            self.dma(self.cb[:], d["consts"][:, 0:384], [], ["cb"], q="gpsimd")
            self.end_phase()
            if "r" in self.phases or "1" in self.phases:
                self.phase_rot()
            for l in range(L):
                xin = d["x"] if l == 0 else d["x2"]
                xout = d["out"] if l == L - 1 else d["x2"]
                if "0" in self.phases:
                    self.phase0(l)
                if "1" in self.phases:
                    self.phase1(l, xin)
                if "2" in self.phases:
                    self.phase2(l)
                if "3" in self.phases:
                    self.phase3(l, xin)
                if "4" in self.phases:
                    self.phase4(l, xout)
            self.end_phase()

    def end_phase(self):
        self.S.barrier()
        self.S.flush()

    def phase_rot(self):
        nc, d = self.nc, self.d
        TWO_PI = float(2 * np.pi)
        PI = float(np.pi)
        with contextlib.ExitStack() as st:
            def sbt(name, shape, dt=F32):
                return st.enter_context(nc.sbuf_tensor(self.uniq(name), list(shape), dt))
            posi = sbt("r_posi", [128, 512], I32)
            ang = sbt("r_ang", [128, 512])
            kf = sbt("r_kf", [128, 512])
            ki = sbt("r_ki", [128, 512], I32)
            r2 = sbt("r_r2", [128, 512])
            sC = [sbt("r_sC%d" % i, [128, 512]) for i in range(2)]
            sS = [sbt("r_sS%d" % i, [128, 512]) for i in range(2)]
            invf = self.cf[:, 384:385]
            sgn = self.cf[:, 385:386]

            def reduce(buf, tag):
                self.ts("vector", kf[:], buf[:], float(1 / TWO_PI), None, ALU.mult, None, [tag], ["kf"])
                self.cp("vector", ki[:], kf[:], ["kf"], ["ki"])
                self.cp("vector", kf[:], ki[:], ["ki"], ["kf"])
                self.stt("vector", buf[:], kf[:], -TWO_PI, buf[:], ALU.mult, ALU.add, ["kf", tag], [tag])
                self.ts("vector", kf[:], buf[:], PI, -TWO_PI, ALU.is_gt, ALU.mult, [tag], ["kf"])
                self.tt("vector", buf[:], buf[:], kf[:], ALU.add, ["kf", tag], [tag])
                self.ts("vector", kf[:], buf[:], -PI, TWO_PI, ALU.is_lt, ALU.mult, [tag], ["kf"])
                self.tt("vector", buf[:], buf[:], kf[:], ALU.add, ["kf", tag], [tag])

            for g in range(NG):
                sl = slice(g * 512, (g + 1) * 512)
                b = g % 2
                self.dma(posi[:], d["pos"][:, sl].partition_broadcast(128), [], ["posi"], sk="posi")
                self.cp("vector", ang[:], posi[:], ["posi"], ["ang"])
                self.ts("vector", ang[:], ang[:], invf, None, ALU.mult, None, ["ang"], ["ang"])
                reduce(ang, "ang")
                self.act(sS[b][:], ang[:], AF.Sin, ["ang"], [("sS", b)])
                self.ts("vector", r2[:], ang[:], PI / 2, None, ALU.add, None, ["ang"], ["r2"])
                reduce(r2, "r2")
                self.act(sC[b][:], r2[:], AF.Sin, ["r2"], [("sC", b)])
                self.act(sS[b][:], sS[b][:], AF.Identity, [("sS", b)], [("sS", b)], scale=sgn)
                self.dma(d["rotC"][:, sl], sC[b][:], [("sC", b)], [], sk=("sC", b))
                self.dma(d["rotS"][:, sl], sS[b][:], [("sS", b)], [], sk=("sS", b))
            self.end_phase()

    def phase0(self, l):
        nc, d = self.nc, self.d
        with contextlib.ExitStack() as st:
            def sbt(name, shape, dt=F32):
                return st.enter_context(nc.sbuf_tensor(self.uniq(name), list(shape), dt))
            c_sb = sbt("a_c", [128, 8])
            cond_bc = sbt("a_cond", [128, 8, 128], BF16)
            wa = [sbt("a_w%d" % i, [128, 8, 512], BF16) for i in range(2)]
            bada = sbt("a_b", [128, 6 * D])
            ada = sbt("a_ada", [128, 6 * D])
            tmp = sbt("a_tmp", [128, 16, 128])
            self.dma(c_sb[:], d["c"], [], ["c_sb"])
            self.dma(bada[:], d["b_ada"][l].partition_broadcast(128), [], ["bada"])
            self.act(c_sb[:], c_sb[:], AF.Silu, ["c_sb"], ["c_sb"])
            self.cp("vector", cond_bc[:], c_sb[:].unsqueeze(2).to_broadcast([128, 8, 128]), ["c_sb"], ["cond"])
            for blk in range(12):
                b = blk % 2
                cs = slice(blk * 512, (blk + 1) * 512)
                self.dma(wa[b][:], d["w_ada"][l][:, :, cs], [], [("wa", b)], q="gpsimd", sk=("wa", b))
                pb = self.ps[b]
                for k in range(8):
                    self.mm(pb[:], cond_bc[:, k, :], wa[b][:, k, :], k == 0, k == 7, ["cond", ("wa", b)], [("ps", b)])
                self.tt("vector", ada[:, cs], pb[:], bada[:, cs], ALU.add, [("ps", b), "bada"], [("ada", blk)])
            allada = [("ada", blk) for blk in range(12)]
            for p_ in range(3):
                self.tt("vector", tmp[:], ada[:, p_ * 2048:(p_ + 1) * 2048].rearrange("p (a b) -> p a b", b=128),
                        self.ident_f.unsqueeze(1).to_broadcast([128, 16, 128]), ALU.mult, allada, ["atmp"])
                self.S.op("vector", (lambda o, i: (lambda e: e.tensor_reduce(o, i, AX.X, ALU.add)))(
                    self.adaT[:, p_ * 16:(p_ + 1) * 16], tmp[:]), ["atmp"], ["adaT"])
            for c0 in (8, 32):
                self.ts("vector", self.adaT[:, c0:c0 + 8], self.adaT[:, c0:c0 + 8], 1.0, None, ALU.add, None, ["adaT"], ["adaT"])
            self.dma(d["ada_row"], ada[0:1, :], allada, [], sk="adast")
            self.end_phase()

    def ln_stats(self, src, tag, stats, mv, rstd, nb, tg):
        S = self.S
        for hf in range(2):
            S.op("vector", (lambda o, i: (lambda e: e.bn_stats(o, i)))(stats[:, hf, :], src[:, hf * 512:(hf + 1) * 512]),
                 [tag], [(tg, "st", hf)])
        S.op("vector", (lambda o, i: (lambda e: e.bn_aggr(o, i)))(mv[:], stats[:]), [(tg, "st", 0), (tg, "st", 1)], [(tg, "mv")])
        self.act(rstd[:], mv[:, 1:2], AF.Sqrt, [(tg, "mv")], [(tg, "rstd")], bias=self.epsb)
        S.op("vector", (lambda o: (lambda e: e.reciprocal(o, o)))(rstd[:]), [(tg, "rstd")], [(tg, "rstd")])
        self.ts("vector", nb[:], mv[:, 0:1], -1.0, rstd[:], ALU.mult, ALU.mult, [(tg, "mv"), (tg, "rstd")], [(tg, "nb")])

    def phase1(self, l, xin):
        nc, S, d = self.nc, self.S, self.d
        ps, psb = self.ps, self.psb
        with contextlib.ExitStack() as st:
            def sbt(name, shape, dt=F32):
                return st.enter_context(nc.sbuf_tensor(self.uniq(name), list(shape), dt))
            w1 = sbt("p1_w1", [128, 8, NC1], BF16)
            b1T = sbt("p1_b1T", [128, NCT])
            b1Ts = sbt("p1_b1Ts", [128, NCT])
            b1row = sbt("p1_b1row", [128, 516])
            wpb = sbt("p1_wpb", [128, 2, 128], BF16)
            pscT = sbt("p1_pscT", [128, 2])
            convT = sbt("p1_convT", [128, 2, 3])
            x_sb = [sbt("p1_x%d" % i, [128, 1024]) for i in range(2)]
            xn = [sbt("p1_xn%d" % i, [128, 1024], BF16) for i in range(2)]
            stats = [sbt("p1_st%d" % i, [128, 2, 6]) for i in range(2)]
            mv = [sbt("p1_mv%d" % i, [128, 2]) for i in range(2)]
            rstd = [sbt("p1_rs%d" % i, [128, 1]) for i in range(2)]
            nb = [sbt("p1_nb%d" % i, [128, 1]) for i in range(2)]
            uT = [sbt("p1_uT%d" % i, [128, 8, 512], BF16) for i in range(2)]
            stage = [sbt("p1_stg%d" % i, [128, 512], BF16) for i in range(4)]
            pbuf = [sbt("p1_pb%d" % i, [128, 528]) for i in range(2)]
            sA = sbt("p1_sA", [128, 528])
            sB = sbt("p1_sB", [128, 528])
            pooled = sbt("p1_pooled", [128, 512], BF16)
            t16 = sbt("p1_t16", [128, 16])
            cbt = sbt("p1_cb", [128, 512])
            cct = sbt("p1_cc", [128, 512])
            cht = sbt("p1_ch", [128, 512])
            vbuf = [sbt("p1_vb%d" % i, [128, 514]) for i in range(2)]
            zt = sbt("p1_z", [128, 512])
            rc = sbt("p1_rc", [128, 512])
            rs = sbt("p1_rsn", [128, 512])
            a1 = sbt("p1_a1", [128, 512])
            a2 = sbt("p1_a2", [128, 512])
            swp = sbt("p1_swp", [128, 4, 512])
            zt2 = sbt("p1_z2", [128, 512])
            vst = [sbt("p1_vst%d" % i, [128, 512], BF16) for i in range(2)]
            tot = sbt("p1_tot", [128, 4, 64])
            zer = sbt("p1_zer", [128, 64])
            invw = self.cf[:, 386:388]
            icnt = self.cf[:, 388:420]

            for k in range(8):
                self.dma(w1[:, k, :], d["w1"][l][:, k, :], [], [("w1", k)], q="gpsimd", sk=("wk", k))
            self.dma(wpb[:], d["wpb"][l], [], ["wpb"], q="gpsimd")
            self.dma(b1T[:], d["b1T"][l], [], ["b1T"])
            self.dma(b1row[:], d["b1row"][l].partition_broadcast(128), [], ["b1row"])
            self.dma(pscT[:], d["pscT"][l], [], ["pscT"])
            self.dma(convT[:], d["convT"][l], [], ["convT"])
            self.ts("vector", b1Ts[:], b1T[:], 0.125, None, ALU.mult, None, ["b1T"], ["b1Ts"])
            for t in range(2):
                self.ms("gpsimd", pbuf[t][:, 0:16], 0.0, [], [("pbuf", t)])
                self.ms("gpsimd", vbuf[t][:, 0:2], 0.0, [], [("vbuf", t)])
            self.ms("gpsimd", zer[:], 0.0, [], ["zer"])
            allw1 = [("w1", k) for k in range(8)]
            state = {"ps": 0, "stg": 0}

            def next_ps():
                state["ps"] = (state["ps"] + 1) % 4
                return state["ps"]

            def next_stage():
                state["stg"] = (state["stg"] + 1) % 4
                return state["stg"]

            def front(g):
                ub = g % 2
                gs = slice(g * 512, (g + 1) * 512)
                uTr = [("uT", ub, i) for i in range(4)]
                for i in range(4):
                    tb = i % 2
                    r0 = g * 512 + i * 128
                    self.dma(x_sb[tb][:], xin[r0:r0 + 128, :], [], [("x", tb)], sk=("x", tb))
                    self.ln_stats(x_sb[tb], ("x", tb), stats[tb], mv[tb], rstd[tb], nb[tb], ("ln", tb))
                    self.act(xn[tb][:], x_sb[tb][:], AF.Identity, [("x", tb), (("ln", tb), "rstd"), (("ln", tb), "nb")], [("xn", tb)],
                             bias=nb[tb][:], scale=rstd[tb][:])
                    for c in range(8):
                        self.tr(psb[tb][:, c * 128:(c + 1) * 128], xn[tb][:, c * 128:(c + 1) * 128], [("xn", tb), "cb"], [("psb", tb)])
                    for c in range(8):
                        self.act(uT[ub][:, c, i * 128:(i + 1) * 128], psb[tb][:, c * 128:(c + 1) * 128], AF.Identity,
                                 [("psb", tb), "adaT"], [("uT", ub, i)], bias=self.adaT[:, c:c + 1], scale=self.adaT[:, 8 + c:9 + c])
                self.dma(d["uT"][:, :, gs], uT[ub][:], uTr, [], sk=("uTst", ub))

            def body(g, part):
                ub = g % 2
                gs = slice(g * 512, (g + 1) * 512)
                uTr = [("uT", ub, i) for i in range(4)]

                def coltile(j, M=128):
                    b = next_ps()
                    for k in range(8):
                        self.mm(ps[b][0:M, :], w1[:, k, j * 128:j * 128 + M], uT[ub][:, k, :], k == 0, k == 7, uTr + allw1, [("ps", b)])
                    return b

                if part == 1:
                    body_b(g, ub, gs, uTr, coltile)
                    return
                self.dma(rc[:], d["rotC"][:, gs], [], ["rc"], sk="rc")
                self.dma(rs[:], d["rotS"][:, gs], [], ["rs"], sk="rs")
                for w_ in range(4):
                    b = coltile(8 + w_)
                    self.act(swp[:, w_, :], ps[b][:], AF.Identity, [("ps", b), "b1T"], [("swp", w_)], bias=b1T[:, 8 + w_:9 + w_])
                for t in range(2):
                    b = coltile(t)
                    pb_ = pbuf[t]
                    pt = ("pbuf", t)
                    self.act(pb_[:, 16:528], ps[b][:], AF.Identity, [("ps", b), "b1T"], [pt], bias=b1T[:, t:t + 1])
                    self.tt("gpsimd", sA[:, 1:528], pb_[:, 1:528], pb_[:, 0:527], ALU.add, [pt], ["sA"])
                    self.tt("gpsimd", sB[:, 3:528], sA[:, 3:528], sA[:, 1:526], ALU.add, ["sA"], ["sB"])
                    if t == 1:
                        self.tt("gpsimd", sA[:, 7:528], sB[:, 7:528], sB[:, 3:524], ALU.add, ["sB", "sA"], ["sA"])
                        self.tt("gpsimd", sB[:, 15:528], sA[:, 15:528], sA[:, 7:520], ALU.add, ["sA", "sB"], ["sB"])
                    lo, hi = sA, sB
                    if g == 0:
                        for (r0_, src_) in ((0, lo), (64, hi)):
                            self.tt("gpsimd", t16[r0_:r0_ + 64, :], src_[r0_:r0_ + 64, 16:32], icnt[r0_:r0_ + 64, t * 16:(t + 1) * 16], ALU.mult,
                                    ["sA", "sB"], ["t16"])
                    self.act(lo[0:64, 16:528], lo[0:64, 16:528], AF.Identity, ["sA", "sB", "t16"], ["sA"], scale=invw[0:64, t:t + 1])
                    self.act(hi[64:128, 16:528], hi[64:128, 16:528], AF.Identity, ["sA", "sB", "t16"], ["sB"], scale=invw[64:128, t:t + 1])
                    self.tt("gpsimd", pooled[0:64, :], lo[0:64, 16:528], pb_[0:64, 16:528], ALU.subtract, ["sA", pt], ["pooled_lo"])
                    self.tt("gpsimd", pooled[64:128, :], hi[64:128, 16:528], pb_[64:128, 16:528], ALU.subtract, ["sB", pt], ["pooled_hi"])
                    if g == 0:
                        self.tt("gpsimd", pooled[:, 0:16], t16[:, :], pb_[:, 16:32], ALU.subtract,
                                ["t16", pt, "pooled_lo", "pooled_hi"], ["pooled_lo", "pooled_hi"])
                    self.cp("gpsimd", pb_[:, 0:16], pb_[:, 512:528], [pt, "pooled_lo", "pooled_hi", "sA", "sB"], [pt])
                    b2 = next_ps()
                    self.mm(ps[b2][:], wpb[:, t, :], pooled[:], True, True, ["pooled_lo", "pooled_hi", "wpb"], [("ps", b2)])
                    sg = next_stage()
                    self.ts("vector", stage[sg][:], ps[b2][:], pscT[:, t:t + 1], None, ALU.mult, None, [("ps", b2), "pscT"], [("stage", sg)])
                    self.dma(d["yT"][t][:, gs], stage[sg][:], [("stage", sg)], [], sk=("stg", sg))
                for t in range(2):
                    vb_ = vbuf[t]
                    vt = ("vbuf", t)
                    for (j, dst, tg) in ((2 + t, cbt, "cbt"), (4 + t, cct, "cct"), (6 + t, cht, "cht")):
                        b = coltile(j)
                        self.act(dst[:], ps[b][:], AF.Identity, [("ps", b), "b1T"], [tg], bias=b1T[:, j:j + 1])
                    self.tt("gpsimd", vb_[:, 2:514], cct[:], cht[:], ALU.mult, ["cct", "cht"], [vt])
                    self.act(zt[:], vb_[:, 2:514], AF.Identity, [vt, "convT"], ["zt"], scale=convT[:, t, 2:3])
                    self.stt("vector", zt[:], vb_[:, 1:513], convT[:, t, 1:2], zt[:], ALU.mult, ALU.add, [vt, "zt", "convT"], ["zt"])
                    self.stt("vector", zt[:], vb_[:, 0:512], convT[:, t, 0:1], zt[:], ALU.mult, ALU.add, [vt, "zt", "convT"], ["zt"])
                    sg = next_stage()
                    self.tt("gpsimd", stage[sg][:], cbt[:], zt[:], ALU.mult, ["cbt", "zt"], [("stage", sg)])
                    self.cp("gpsimd", vb_[:, 0:2], vb_[:, 512:514], [vt], [vt])
                    self.dma(d["yT"][2 + t][:, gs], stage[sg][:], [("stage", sg)], [], sk=("stg", sg))
            def body_b(g, ub, gs, uTr, coltile):
                for qi in range(8):
                    j = 12 + qi
                    is_q = qi in (0, 1, 4, 5)
                    is_moba = qi < 4
                    sc_ = 0.125 if is_q else 1.0
                    bt = b1Ts if is_q else b1T
                    b = coltile(j)
                    sg = next_stage()
                    self.act(stage[sg][:], ps[b][:], AF.Identity, [("ps", b), "b1T", "b1Ts"], [("stage", sg)], bias=bt[:, j:j + 1], scale=sc_)
                    if is_moba:
                        w_ = qi
                        for hh in range(2):
                            r0_ = hh * 64
                            self.stt("vector", a1[r0_:r0_ + 16, :], ps[b][r0_:r0_ + 16, :], b1T[r0_:r0_ + 16, j:j + 1], rc[r0_:r0_ + 16, :],
                                     ALU.add, ALU.mult, [("ps", b), "b1T", "rc"], ["a1"])
                            self.tt("gpsimd", a2[r0_:r0_ + 16, :], swp[r0_:r0_ + 16, w_, :], rs[r0_:r0_ + 16, :], ALU.mult, [("swp", w_), "rs"], ["a2"])
                            self.tt("gpsimd", a2[r0_:r0_ + 16, :], a2[r0_:r0_ + 16, :], a1[r0_:r0_ + 16, :], ALU.add, ["a1", "a2"], ["a2"])
                            self.act(stage[sg][r0_:r0_ + 16, :], a2[r0_:r0_ + 16, :], AF.Copy, ["a2"], [("stage", sg)], scale=sc_)
                    self.dma(d["qk"][qi][:, gs], stage[sg][:], [("stage", sg)], [], sk=("stg", sg))
                bF = next_ps()
                for i in range(4):
                    b = next_ps()
                    if b == bF:
                        b = next_ps()
                    vs_ = i % 2
                    for k in range(8):
                        self.mm(ps[b][:], uT[ub][:, k, i * 128:(i + 1) * 128], w1[:, k, 2560:3072], k == 0, k == 7, uTr + allw1, [("ps", b)])
                    self.tt("vector", vst[vs_][:], ps[b][:], b1row[:, 0:512], ALU.add, [("ps", b), "b1row"], [("vst", vs_)])
                    r0 = g * 512 + i * 128
                    self.dma(d["v"][r0:r0 + 128, :], vst[vs_][:], [("vst", vs_)], [], sk=("vst", vs_))
                    for k in range(8):
                        self.mm(ps[bF][:, i * 4:(i + 1) * 4], uT[ub][:, k, i * 128:(i + 1) * 128], w1[:, k, 3072:3076], k == 0, k == 7,
                                uTr + allw1, [("ps", bF)])
                self.tt("vector", self.lall[:, g * 4:(g + 1) * 4, :], ps[bF][:, 0:16].rearrange("p (a b) -> p a b", b=4),
                        b1row[:, 512:516].unsqueeze(1).to_broadcast([128, 4, 4]), ALU.add, [("ps", bF), "b1row"], [("lall", g)])

            for g in range(NG):
                front(g)
                body(g, 0)
                body(g, 1)
            lr = [("lall", g) for g in range(NG)]
            lflat = self.lall[:].rearrange("p a b -> p (a b)")
            self.act(lflat, lflat, AF.Exp, lr, ["lall2"], scale=-1.0)
            self.act(lflat, lflat, AF.Ln, ["lall2"], ["lall3"], bias=1.0)
            self.mm(ps[0][:, 0:256], self.U_f, lflat, True, True, ["lall3"], [("ps", 0)])
            self.mm(ps[1][:, 0:256], self.ones_f, lflat, True, True, ["lall3"], [("ps", 1)])
            self.cp("vector", tot[:], ps[1][:, 0:256].rearrange("p (t h) -> p h t", h=4), [("ps", 1)], ["tot"])
            for h in range(4):
                S.op("vector", (lambda o, a, z: (lambda e: e.tensor_tensor_scan(o, a, z, 0.0, ALU.add, ALU.add)))(
                    self.incl[:, h, :], tot[:, h, :], zer[:]), ["tot", "zer"], [("incl", h)])
            inclr = [("incl", h) for h in range(4)]
            self.tt("vector", tot[:], self.incl[:], tot[:], ALU.subtract, inclr + ["tot"], ["tot"])
            self.tt("vector", self.cl[:], ps[0][:, 0:256].rearrange("p (t h) -> p h t", h=4), tot[:], ALU.add, [("ps", 0), "tot"], ["cl"])
            self.tt("vector", self.cmid[:], tot[:], self.incl[:], ALU.add, inclr + ["tot"], ["cmid"])
            self.ts("vector", self.cmid[:], self.cmid[:], 0.5, None, ALU.mult, None, ["cmid"], ["cmid"])
            if self.debug:
                self.dma(d["dbg"][:, 0:256], self.cl[:].rearrange("p a b -> p (a b)"), ["cl"], [], sk="dbg1")
                self.dma(d["dbg"][:, 256:512], self.incl[:].rearrange("p a b -> p (a b)"), inclr, [], sk="dbg2")
            self.end_phase()

    def phase2(self, l):
        nc, S, d = self.nc, self.S, self.d
        ps = self.ps
        with contextlib.ExitStack() as st:
            def sbt(name, shape, dt=F32):
                return st.enter_context(nc.sbuf_tensor(self.uniq(name), list(shape), dt))
            kT = [sbt("p2_kT%d" % i, [128, T], BF16) for i in range(2)]
            V = [sbt("p2_V%d" % i, [128, NTL, 65], BF16) for i in range(2)]
            qT = [sbt("p2_qT%d" % i, [128, 512], BF16) for i in range(2)]
            NPT = 6
            pT = [sbt("p2_pT%d" % i, [128, 512], BF16) for i in range(NPT)]
            o_sb = sbt("p2_o", [64, 512])
            rcr = sbt("p2_rc", [128, 512])
            y_sb = [sbt("p2_y%d" % i, [64, 512], BF16) for i in range(2)]
            sc_sb = [sbt("p2_sc%d" % i, [128, 4, 32]) for i in range(2)]
            m8 = [sbt("p2_m8%d" % i, [128, 4, 8]) for i in range(2)]
            mk = [sbt("p2_mk%d" % i, [128, 4, 96], BF16) for i in range(2)]
            kbf = sbt("p2_kbf", [64, 32])
            kbarT = [sbt("p2_kb%d" % i, [64, 32], BF16) for i in range(2)]
            biasT = [sbt("p2_bias%d" % i, [128, 64]) for i in range(2)]
            psS = [ps[0], ps[1], ps[2], self.psb[1][:].bitcast(F32)]
            NPS = len(psS)
            psO = [ps[3], ps[4]]
            psX = ps[5]
            psM = self.psb[0]

            for i in range(2):
                self.dma(kT[i][64:96, :], d["onehot"], [], [("kToh", i)], q="gpsimd")
            for i in range(2):
                self.ms("gpsimd", V[i][:, :, 64:65], 1.0, [], [("Vone", i)])
                self.ms("gpsimd", mk[i][:], 0.0, [], [("mk", i)])
                self.ms("gpsimd", kT[i][96:97, :], 1.0, [], [("kToh", i)])

            heads = [(ty, h) for ty in range(2) for h in range(4)]
            qcount = [0]

            def load_head(hi):
                ty, h = heads[hi]
                bi = hi % 2
                ktile = (2 if ty == 0 else 6) + h // 2
                r0 = (h % 2) * 64
                self.dma(kT[bi][0:64, :], d["qk"][ktile][r0:r0 + 64, :], [], [("kT", bi)], sk=("kT", bi))
                c0 = ty * 256 + h * 64
                self.dma(V[bi][:, :, 0:64], d["v"][:, c0:c0 + 64].rearrange("(n p) c -> p n c", p=128), [], [("V", bi)], sk=("V", bi))

            def load_q(hi, g):
                ty, h = heads[hi]
                qb = qcount[0] % 2
                qcount[0] += 1
                qtile = (0 if ty == 0 else 4) + h // 2
                r0 = (h % 2) * 64
                self.dma(qT[qb][0:64, :], d["qk"][qtile][r0:r0 + 64, g * 512:(g + 1) * 512], [], [("qT", qb)], sk=("qT", qb))
                return qb

            def prep(hi, g, qb):
                ty, h = heads[hi]
                bi = hi % 2
                if ty == 0:
                    self.ms("gpsimd", sc_sb[qb][:], -1e30, [], [("sc", qb)])
                    for i in range(4):
                        self.mm(psX[:, i * 32:(i + 1) * 32], qT[qb][0:64, i * 128:(i + 1) * 128], kbarT[bi][:, :], True, True,
                                [("qT", qb), ("kbar", bi)], ["psX"])
                    for i in range(4):
                        own = (4 * g + i) // 2
                        if own > 0:
                            self.cp("vector", sc_sb[qb][:, i, 0:own], psX[:, i * 32:i * 32 + own], ["psX", ("sc", qb)], [("sc", qb)])
                    for i in range(4):
                        S.op("vector", (lambda o, a: (lambda e: e.max(o, a)))(m8[qb][:, i, :], sc_sb[qb][:, i, :]), [("sc", qb)], [("m8", qb)])
                    for i in range(4):
                        self.ts("vector", mk[qb][:, i, 64:96], sc_sb[qb][:, i, :], m8[qb][:, i, 2:3], NEG, ALU.is_lt, ALU.mult,
                                [("sc", qb), ("m8", qb)], [("mk", qb)])
                    for i in range(4):
                        own = (4 * g + i) // 2
                        self.ms("gpsimd", mk[qb][:, i, 64 + own:65 + own], 0.0, [("mk", qb)], [("mk", qb)])
                    for i in range(4):
                        self.tr(psM[0:96, i * 128:(i + 1) * 128], mk[qb][:, i, 0:96], [("mk", qb), "cb"], ["psM"])
                    self.act(qT[qb][64:96, :], psM[64:96, 0:512], AF.Copy, ["psM", ("qT", qb)], [("qT", qb)])
                else:
                    n = 4 * g + 4
                    self.ts("vector", biasT[qb][:, 0:n], self.cl[:, h, 0:n], self.incl[:, h, 4 * g + 1:4 * g + 2], None, ALU.subtract, None,
                            [], [("biasT", qb)])
                    if h == 0 and g < 2:
                        self.ms("gpsimd", qT[qb][64:96, :], 0.0, [("qT", qb)], [("qT", qb)])
                    self.ts("vector", qT[qb][96:97, :].rearrange("p (a b) -> p a b", b=128),
                            self.cmid[96:97, h, 4 * g:4 * g + 4].unsqueeze(2).to_broadcast([1, 4, 128]),
                            -1.0, self.incl[96:97, h, 4 * g + 1:4 * g + 2], ALU.mult, ALU.add, [("qT", qb)], [("qT", qb)])

            def main(hi, g, qb):
                ty, h = heads[hi]
                bi = hi % 2
                K = 96 if ty == 0 else 97
                nkt = 4 * g + 4
                ob = g % 2

                def emit_S(kt):
                    nq0 = max(0, kt - 4 * g) * 128
                    sb_ = kt % NPS
                    pb_ = kt % NPT
                    self.mm(psS[sb_][:, nq0:512], kT[bi][0:K, kt * 128:(kt + 1) * 128], qT[qb][0:K, nq0:512], True, True,
                            [("kT", bi), ("kToh", bi), ("qT", qb)], [("psS", sb_)])
                    if ty == 0:
                        self.act(pT[pb_][:, nq0:512], psS[sb_][:, nq0:512], AF.Exp, [("psS", sb_)], [("pT", pb_)])
                    else:
                        self.act(pT[pb_][:, nq0:512], psS[sb_][:, nq0:512], AF.Exp, [("psS", sb_), ("biasT", qb)], [("pT", pb_)],
                                 bias=biasT[qb][:, kt:kt + 1])
                    if kt >= 4 * g:
                        self.tt("gpsimd", pT[pb_][:, nq0:nq0 + 128], pT[pb_][:, nq0:nq0 + 128], self.tri_b, ALU.mult, [("pT", pb_)], [("pT", pb_)])

                def emit_PV(kt):
                    nq0 = max(0, kt - 4 * g) * 128
                    pb_ = kt % NPT
                    self.mm(psO[ob][0:65, nq0:512], V[bi][:, kt, 0:65], pT[pb_][:, nq0:512], kt == 0, kt == nkt - 1,
                            [("pT", pb_), ("V", bi), ("Vone", bi)], [("psO", ob)])

                SK = 3
                for kt in range(nkt + SK):
                    if kt < nkt:
                        emit_S(kt)
                    if kt >= SK:
                        emit_PV(kt - SK)
                self.act(o_sb[:], psO[ob][0:64, :], AF.Copy, [("psO", ob)], ["o_sb"])
                S.op("vector", (lambda o, a: (lambda e: e.reciprocal(o, a)))(rcr[64:65, :], psO[ob][64:65, :]), [("psO", ob)], ["rcr"])
                self.mm(psX[0:64, :], self.ones_f[64:65, 0:64], rcr[64:65, :], True, True, ["rcr"], ["psX"])
                yb = g % 2
                self.tt("vector", y_sb[yb][:], o_sb[:], psX[0:64, :], ALU.mult, ["o_sb", "psX"], [("y", yb)])
                ytile = 4 + ty * 2 + h // 2
                r0 = (h % 2) * 64
                self.dma(d["yT"][ytile][r0:r0 + 64, g * 512:(g + 1) * 512], y_sb[yb][:], [("y", yb)], [], sk=("y", yb))

            load_head(0)
            for hi in range(8):
                ty, h = heads[hi]
                bi = hi % 2
                if hi + 1 < 8:
                    load_head(hi + 1)
                if ty == 0:
                    S.op("vector", (lambda o, a: (lambda e: e.tensor_reduce(o, a, AX.X, ALU.add)))(
                        kbf[:], kT[bi][0:64, :].rearrange("p (n c) -> p n c", c=256)), [("kT", bi)], ["kbf"])
                    self.ts("vector", kbarT[bi][:], kbf[:], 1.0 / 256.0, None, ALU.mult, None, ["kbf"], [("kbar", bi)])
                qb = load_q(hi, 0)
                prep(hi, 0, qb)
                for g in range(NG):
                    qb_n = None
                    if g + 1 < NG:
                        qb_n = load_q(hi, g + 1)
                        prep(hi, g + 1, qb_n)
                    main(hi, g, qb)
                    qb = qb_n
            self.end_phase()

    def residual_ln(self, pfx, pA, pB, tagA, tagB, xt, tagx, gbc, lng, lnb, tmp, stats, mv, rstd, nb, dst_ap):
        t = (pfx, "tmp")
        self.tt("vector", tmp[:, 0:512], pA[:], gbc[:, 0:512], ALU.mult, [tagA, (pfx, "gbc")], [(pfx, "tmpa"), t])
        self.tt("vector", tmp[:, 512:1024], pB[:], gbc[:, 512:1024], ALU.mult, [tagB, (pfx, "gbc")], [(pfx, "tmpb"), t])
        self.act(xt[:], xt[:], AF.Copy, [tagx], [tagx], scale=float(ALPHA))
        self.tt("gpsimd", tmp[:], tmp[:], xt[:], ALU.add, [(pfx, "tmpa"), (pfx, "tmpb"), tagx], [t])
        self.ln_stats(tmp, t, stats, mv, rstd, nb, (pfx, "ln"))
        self.act(tmp[:], tmp[:], AF.Identity, [t, ((pfx, "ln"), "rstd"), ((pfx, "ln"), "nb")], [t], bias=nb[:], scale=rstd[:])
        self.tt("gpsimd", tmp[:], tmp[:], lng[:], ALU.mult, [t, (pfx, "lng")], [t])
        self.tt("gpsimd", tmp[:], tmp[:], lnb[:], ALU.add, [t, (pfx, "lnb")], [t])
        self.dma(dst_ap, tmp[:], [t], [], sk=("xst", pfx))

    def phase3(self, l, xin):
        nc, S, d = self.nc, self.S, self.d
        ps = self.ps
        with contextlib.ExitStack() as st:
            def sbt(name, shape, dt=F32):
                return st.enter_context(nc.sbuf_tensor(self.uniq(name), list(shape), dt))
            w2 = sbt("p3_w2", [128, 8, 4096], BF16)
            wbr = sbt("p3_wbr", [128, 8, D], BF16)
            wo = sbt("p3_wo", [128, 8, D], BF16)
            b2T = sbt("p3_b2T", [128, 32])
            g1 = sbt("p3_g1", [128, D])
            lng = sbt("p3_lng", [128, D])
            lnb = sbt("p3_lnb", [128, D])
            uTs = [sbt("p3_uT%d" % i, [128, 8, 512], BF16) for i in range(2)]
            yTs = [sbt("p3_yT%d" % i, [128, 8, 512], BF16) for i in range(2)]
            xt = [sbt("p3_xt%d" % i, [128, D]) for i in range(2)]
            mT = sbt("p3_mT", [128, 8, 512], BF16)
            gate = [sbt("p3_gate%d" % i, [128, 512]) for i in range(2)]
            term = [sbt("p3_term%d" % i, [128, 512]) for i in range(2)]
            acc = sbt("p3_acc", [128, 512])
            tmp = [sbt("p3_tmp%d" % i, [128, D]) for i in range(2)]
            stats = [sbt("p3_st%d" % i, [128, 2, 6]) for i in range(2)]
            mv = [sbt("p3_mv%d" % i, [128, 2]) for i in range(2)]
            rstd = [sbt("p3_rs%d" % i, [128, 1]) for i in range(2)]
            nb = [sbt("p3_nb%d" % i, [128, 1]) for i in range(2)]
            for k in range(8):
                self.dma(w2[:, k, :], d["w2"][l][:, k, :], [], [("w2", k)], q="gpsimd", sk=("wk", k))
            self.dma(wbr[:], d["wbr"][l], [], ["wbr"], q="gpsimd")
            self.dma(wo[:], d["wo"][l], [], ["wo"], q="gpsimd")
            self.dma(b2T[:], d["b2T"][l], [], ["b2T"])
            self.dma(g1[:], d["ada_row"][:, 2048:3072].partition_broadcast(128), [], [("p3a", "gbc"), ("p3b", "gbc")])
            self.dma(lng[:], d["lnrow"][l][0:1, :].partition_broadcast(128), [], [("p3a", "lng"), ("p3b", "lng")])
            self.dma(lnb[:], d["lnrow"][l][1:2, :].partition_broadcast(128), [], [("p3a", "lnb"), ("p3b", "lnb")])
            allw2 = [("w2", k) for k in range(8)]
            cnt = [0]

            def load(g):
                gs = slice(g * 512, (g + 1) * 512)
                sl_ = g % 2
                self.dma(uTs[sl_][:], d["uT"][:, :, gs], [], [("uT", sl_)], sk=("p3uT", sl_))
                self.dma(yTs[sl_][:], d["yT"][:, :, gs].rearrange("n p t -> p n t"), [], [("yT", sl_)], sk=("p3yT", sl_))

            def group(g):
                sl_ = g % 2
                uT, yT = uTs[sl_], yTs[sl_]
                if g + 1 < NG:
                    load(g + 1)
                for dc in range(8):
                    for n in range(4):
                        c_ = cnt[0] % 2
                        cnt[0] += 1
                        pg = ps[c_]
                        pbr = ps[2 + c_]
                        for k in range(8):
                            self.mm(pg[:], w2[:, k, n * 1024 + dc * 128:n * 1024 + (dc + 1) * 128], uT[:, k, :], k == 0, k == 7,
                                    [("uT", sl_)] + allw2, [("psg", c_)])
                        for c in range(2):
                            self.mm(pbr[:], wbr[:, n * 2 + c, dc * 128:(dc + 1) * 128], yT[:, n * 2 + c, :], c == 0, c == 1,
                                    [("yT", sl_), "wbr"], [("psbr", c_)])
                        self.act(gate[c_][:], pg[:], AF.Sigmoid, [("psg", c_), "b2T"], [("gate", c_)], bias=b2T[:, n * 8 + dc:n * 8 + dc + 1])
                        if n == 0:
                            self.tt("vector", acc[:], gate[c_][:], pbr[:], ALU.mult, [("gate", c_), ("psbr", c_)], ["acc"])
                        else:
                            self.tt("vector", term[c_][:], gate[c_][:], pbr[:], ALU.mult, [("gate", c_), ("psbr", c_)], [("term", c_)])
                            if n < 3:
                                self.tt("gpsimd", acc[:], acc[:], term[c_][:], ALU.add, ["acc", ("term", c_)], ["acc"])
                            else:
                                self.tt("gpsimd", mT[:, dc, :], acc[:], term[c_][:], ALU.add, ["acc", ("term", c_)], [("mT", dc)])
                mTr = [("mT", dc) for dc in range(8)]
                for i in range(4):
                    tb = i % 2
                    pfx = "p3a" if tb == 0 else "p3b"
                    r0 = g * 512 + i * 128
                    self.dma(xt[tb][:], xin[r0:r0 + 128, :], [], [("p3x", tb)], sk=("p3x", tb))
                    pA, pB = ps[4], ps[5]
                    for hf, pp, tg in ((0, pA, "p3psA"), (1, pB, "p3psB")):
                        for k in range(8):
                            self.mm(pp[:], mT[:, k, i * 128:(i + 1) * 128], wo[:, k, hf * 512:(hf + 1) * 512], k == 0, k == 7,
                                    mTr + ["wo"], [tg])
                    self.residual_ln(pfx, pA, pB, "p3psA", "p3psB", xt[tb], ("p3x", tb), g1, lng, lnb, tmp[tb], stats[tb], mv[tb],
                                     rstd[tb], nb[tb], d["x1"][r0:r0 + 128, :])

            load(0)
            for g in range(NG):
                group(g)
            self.end_phase()

    def phase4(self, l, xout):
        nc, S, d = self.nc, self.S, self.d
        ps, psb = self.ps, self.psb
        with contextlib.ExitStack() as st:
            def sbt(name, shape, dt=F32):
                return st.enter_context(nc.sbuf_tensor(self.uniq(name), list(shape), dt))
            wup = sbt("p4_wup", [128, 8, DFF], BF16)
            wdn = sbt("p4_wdn", [128, 32, D], BF16)
            bupT = sbt("p4_bupT", [128, 32])
            g2 = sbt("p4_g2", [128, D])
            lng = sbt("p4_lng", [128, D])
            lnb = sbt("p4_lnb", [128, D])
            u2T = sbt("p4_u2T", [128, 8, 512], BF16)
            hT = sbt("p4_hT", [128, 32, 512], BF16)
            xt = sbt("p4_xt", [128, D])
            xn2 = [sbt("p4_xn%d" % i, [128, D], BF16) for i in range(2)]
            xtf1 = sbt("p4_xtf1", [128, D])
            rt = sbt("p4_rt", [128, 512])
            tmp = sbt("p4_tmp", [128, D])
            stats = sbt("p4_st", [128, 2, 6])
            mv = sbt("p4_mv", [128, 2])
            rstd = sbt("p4_rs", [128, 1])
            nb = sbt("p4_nb", [128, 1])
            for k in range(8):
                self.dma(wup[:, k, :], d["wup"][l][:, k, :], [], [("wup", k)], q="gpsimd", sk=("wk", k))
            for k in range(0, 32, 4):
                self.dma(wdn[:, k:k + 4, :], d["wdn"][l][:, k:k + 4, :], [], [("wdn", k)], q="gpsimd", sk=("wk2", k))
            self.dma(bupT[:], d["bupT"][l], [], ["bupT"])
            self.dma(g2[:], d["ada_row"][:, 5120:6144].partition_broadcast(128), [], [("p4", "gbc")])
            self.dma(lng[:], d["lnrow"][l][2:3, :].partition_broadcast(128), [], [("p4", "lng")])
            self.dma(lnb[:], d["lnrow"][l][3:4, :].partition_broadcast(128), [], [("p4", "lnb")])
            allwup = [("wup", k) for k in range(8)]
            allwdn = [("wdn", k) for k in range(0, 32, 4)]

            def group(g):
                for i in range(4):
                    tb = i % 2
                    r0 = g * 512 + i * 128
                    xsrc = xt if tb == 0 else xtf1
                    xtag = "p4x" if tb == 0 else "p4x1"
                    xn = xn2[tb]
                    self.dma(xsrc[:], d["x1"][r0:r0 + 128, :], [], [xtag], sk=xtag)
                    self.ln_stats(xsrc, xtag, stats, mv, rstd, nb, ("p4", "ln"))
                    self.act(xn[:], xsrc[:], AF.Identity, [xtag, (("p4", "ln"), "rstd"), (("p4", "ln"), "nb")], [("p4xn", tb)],
                             bias=nb[:], scale=rstd[:])
                    for c in range(8):
                        self.tr(psb[tb][:, c * 128:(c + 1) * 128], xn[:, c * 128:(c + 1) * 128], [("p4xn", tb), "cb"], [("psb", tb)])
                    for c in range(8):
                        self.act(u2T[:, c, i * 128:(i + 1) * 128], psb[tb][:, c * 128:(c + 1) * 128], AF.Identity,
                                 [("psb", tb), "adaT"], [("u2T", i)], bias=self.adaT[:, 24 + c:25 + c], scale=self.adaT[:, 32 + c:33 + c])
                u2r = [("u2T", i) for i in range(4)]
                for fc in range(32):
                    c_ = fc % 2
                    pu = ps[c_]
                    for k in range(8):
                        self.mm(pu[:], wup[:, k, fc * 128:(fc + 1) * 128], u2T[:, k, :], k == 0, k == 7, u2r + allwup, [("psu", c_)])
                    self.ts("vector", rt[:], pu[:], bupT[:, fc:fc + 1], 0.0, ALU.add, ALU.max, [("psu", c_), "bupT"], ["rt"])
                    self.act(hT[:, fc, :], rt[:], AF.Square, ["rt"], [("hT", fc)])
                hr = [("hT", fc) for fc in range(32)]
                for i in range(4):
                    r0 = g * 512 + i * 128
                    par = i % 2
                    pA, pB = ps[2 + par * 2], ps[3 + par * 2]
                    tgA, tgB = ("p4psA", par), ("p4psB", par)
                    for hf, pp, tg in ((0, pA, tgA), (1, pB, tgB)):
                        for fc in range(32):
                            self.mm(pp[:], hT[:, fc, i * 128:(i + 1) * 128], wdn[:, fc, hf * 512:(hf + 1) * 512], fc == 0, fc == 31,
                                    hr + allwdn, [tg])
                    self.dma(xt[:], d["x1"][r0:r0 + 128, :], [], ["p4x"], sk="p4x")
                    self.residual_ln("p4", pA, pB, tgA, tgB, xt, "p4x", g2, lng, lnb, tmp, stats, mv, rstd, nb, xout[r0:r0 + 128, :])

            for g in range(NG):
                group(g)
            self.end_phase()


def prep_weights(w_ada, b_ada, w_in, b_in, w_pool, pool_scale, conv_w, w_branch, w_o, ln1_g, ln1_b,
                 w_up, b_up, w_down, ln2_g, ln2_b, depth=DEPTH):
    L = depth
    f = lambda a: np.ascontiguousarray(np.asarray(a, dtype=np.float32))
    w_in = f(w_in)
    b_in = f(b_in)
    cols = _cols1()
    m = {}
    m["w_ada"] = f(f(w_ada)[:L].reshape(L, 8, 128, 6 * D).transpose(0, 2, 1, 3))
    m["b_ada"] = f(f(b_ada)[:L].reshape(L, 1, 6 * D))
    m["w1"] = f(w_in[:L][:, :, cols].reshape(L, 8, 128, NC1).transpose(0, 2, 1, 3))
    m["b1T"] = f(b_in[:L][:, cols[:NCT * 128]].reshape(L, NCT, 128).transpose(0, 2, 1))
    m["b1row"] = f(b_in[:L][:, cols[NCT * 128:]].reshape(L, 1, 516))
    m["w2"] = f(w_in[:L][:, :, 2564:].reshape(L, 8, 128, 4096).transpose(0, 2, 1, 3))
    m["b2T"] = f(b_in[:L][:, 2564:].reshape(L, 32, 128).transpose(0, 2, 1))
    wp = f(w_pool)[:L]
    wpb = np.zeros((L, 128, 2, 128), np.float32)
    for t in range(2):
        wpb[:, 0:64, t, 0:64] = wp[:, 2 * t]
        wpb[:, 64:128, t, 64:128] = wp[:, 2 * t + 1]
    m["wpb"] = wpb
    m["pscT"] = f(f(pool_scale)[:L].reshape(L, 2, 128).transpose(0, 2, 1))
    m["convT"] = f(f(conv_w)[:L].reshape(L, 3, 2, 128).transpose(0, 3, 2, 1))
    m["wbr"] = f(f(w_branch)[:L].reshape(L, 4, 2, 128, D).transpose(0, 3, 1, 2, 4).reshape(L, 128, 8, D))
    m["wo"] = f(f(w_o)[:L].reshape(L, 8, 128, D).transpose(0, 2, 1, 3))
    m["lnrow"] = f(np.stack([f(ln1_g)[:L], f(ln1_b)[:L], f(ln2_g)[:L], f(ln2_b)[:L]], axis=1))
    m["wup"] = f(f(w_up)[:L].reshape(L, 8, 128, DFF).transpose(0, 2, 1, 3))
    m["bupT"] = f(f(b_up)[:L].reshape(L, 32, 128).transpose(0, 2, 1))
    m["wdn"] = f(f(w_down)[:L].reshape(L, 32, 128, D).transpose(0, 2, 1, 3))
    m["consts"] = _consts()
    m["onehot"] = _onehot()
    return m


_PROG_CACHE = {}


def get_prog(depth=DEPTH, debug=False, phases="01234"):
    key = (depth, debug, phases)
    if key not in _PROG_CACHE:
        _PROG_CACHE[key] = Prog(depth, debug, phases)
    return _PROG_CACHE[key]


def kernel(x, c, positions, w_ada, b_ada, w_in, b_in, w_pool, pool_scale, conv_w, w_branch, w_o,
           ln1_g, ln1_b, w_up, b_up, w_down, ln2_g, ln2_b):
    prog = get_prog()
    wm = prep_weights(w_ada, b_ada, w_in, b_in, w_pool, pool_scale, conv_w, w_branch, w_o, ln1_g, ln1_b,
                      w_up, b_up, w_down, ln2_g, ln2_b)
    x = np.asarray(x, dtype=np.float32)
    c = np.asarray(c, dtype=np.float32)
    positions = np.asarray(positions, dtype=np.int32)
    in_maps = []
    for b in range(BATCH):
        m = dict(wm)
        m["x"] = np.ascontiguousarray(x[b])
        m["c"] = np.ascontiguousarray(c[b].reshape(8, 128).T)
        m["pos"] = np.ascontiguousarray(positions[b].reshape(1, T))
        in_maps.append(m)
    res = run_bass_kernel_spmd(prog.nc, in_maps, core_ids=list(range(BATCH)))
    out = np.stack([np.asarray(res.results[b]["out"], dtype=np.float32) for b in range(BATCH)], axis=0)
    return out
```
